# Optimizing a Trainium2 kernel written in Bass

```python
import jax
import jax.numpy as jnp
from jax import lax
import numpy as np

D_MODEL = 2048
BATCH = 32
SEQ = 256
DEPTH = 2
DEC_BATCH = 8
DEC_SEQ = 4096
PAST_LEN = 512

GRID_W = 64
HEAD_DIM = 64
A_HEADS = 12
A_KV_HEADS = 4
A_GROUP = A_HEADS // A_KV_HEADS
A_WIDTH = A_HEADS * HEAD_DIM
A_KV_WIDTH = A_KV_HEADS * HEAD_DIM
A_WINDOW = 128
A_BLOCK = 128
B_WIDTH = 512
HYENA_ORDER = 2
HYENA_BANDS = 16
HYENA_EMB = 1 + 2 * HYENA_BANDS
HYENA_HIDDEN = 64
SHORT_CONV = 3
C_HEADS = 12
C_WIDTH = C_HEADS * HEAD_DIM
NA_ROWS = 8
NA_COLS = 16
Q_BLOCK = 128
ROPE_BASE = 10000.0
EPS = 1e-6
NEG_INF = -1e30
N_BRANCH = 3
IN_SPLITS = (A_WIDTH, A_KV_WIDTH, A_KV_WIDTH, A_WIDTH, 3 * B_WIDTH, B_WIDTH, 3 * C_WIDTH, C_WIDTH, N_BRANCH * D_MODEL)
IN_WIDTH = sum(IN_SPLITS)
IN_OFFSETS = tuple(int(o) for o in np.cumsum(IN_SPLITS)[:-1])

kernel_name = "hybrid_diffusion_prefix_trunk_step"

F32 = jnp.float32


def rms_norm(x, w):
    xf = x.astype(F32)
    y = xf * lax.rsqrt(jnp.mean(xf * xf, axis=-1, keepdims=True) + EPS)
    return (y * w.astype(F32)).astype(x.dtype)


def axial_rope(L):
    t = jnp.arange(L)
    row = (t // GRID_W).astype(F32)
    col = (t % GRID_W).astype(F32)
    nf = HEAD_DIM // 4
    inv = jnp.power(ROPE_BASE, -jnp.arange(nf, dtype=F32) / nf)
    ang = jnp.concatenate([row[:, None] * inv[None], col[:, None] * inv[None]], axis=-1)
    return jnp.cos(ang), jnp.sin(ang)


def apply_rope(x, cos, sin):
    xf = x.astype(F32)
    half = HEAD_DIM // 2
    x1, x2 = xf[..., :half], xf[..., half:]
    c = cos[None, :, None, :]
    s = sin[None, :, None, :]
    return jnp.concatenate([x1 * c - x2 * s, x1 * s + x2 * c], axis=-1).astype(x.dtype)


def project(x, cond, p):
    mod = jnp.matmul(jax.nn.silu(cond), p["w_ada"]) + p["b_ada"]
    shift, scale, gate = jnp.split(mod, 3, axis=-1)
    h = rms_norm(x, p["norm_w"]) * (1 + scale) + shift
    u = jnp.matmul(h, p["w_in"])
    return jnp.split(u, list(IN_OFFSETS), axis=-1), gate


def context_attn(q, k, v, sink):
    Bn, S, Hk, G, dh = q.shape
    nb = S // Q_BLOCK
    scale = dh ** -0.5
    qb = jnp.moveaxis(q.reshape(Bn, nb, Q_BLOCK, Hk, G, dh), 1, 0)

    def block(qi):
        s = jnp.einsum('bqkgd,bskd->bkgqs', qi, k, preferred_element_type=F32) * scale
        if sink is not None:
            sk = jnp.broadcast_to(sink.astype(F32)[None, :, :, None, None], s.shape[:-1] + (1,))
            pr = jax.nn.softmax(jnp.concatenate([s, sk], axis=-1), axis=-1)[..., :S]
        else:
            pr = jax.nn.softmax(s, axis=-1)
        return jnp.einsum('bkgqs,bskd->bqkgd', pr.astype(v.dtype), v)

    o = lax.map(block, qb)
    return jnp.moveaxis(o, 0, 1).reshape(Bn, S, Hk * G * dh)


def window_attn_latent(q, k, v, ctx_k, ctx_v, sink):
    Bn, L, Hk, G, dh = q.shape
    P = ctx_k.shape[1]
    nb = L // A_BLOCK
    span = A_BLOCK + 2 * A_WINDOW
    scale = dh ** -0.5
    pad = ((0, 0), (A_WINDOW, A_WINDOW), (0, 0), (0, 0))
    kp = jnp.pad(k, pad)
    vp = jnp.pad(v, pad)
    qb = jnp.moveaxis(q.reshape(Bn, nb, A_BLOCK, Hk, G, dh), 1, 0)
    sink32 = sink.astype(F32)[None, :, :, None, None]

    def block(args):
        i, qi = args
        start = i * A_BLOCK
        kb = lax.dynamic_slice_in_dim(kp, start, span, axis=1)
        vb = lax.dynamic_slice_in_dim(vp, start, span, axis=1)
        qpos = start + jnp.arange(A_BLOCK)
        kpos = start - A_WINDOW + jnp.arange(span)
        valid = (jnp.abs(qpos[:, None] - kpos[None, :]) <= A_WINDOW) & (kpos >= 0)[None, :] & (kpos < L)[None, :]
        s_loc = jnp.einsum('bqkgd,bskd->bkgqs', qi, kb, preferred_element_type=F32) * scale
        s_loc = jnp.where(valid, s_loc, NEG_INF)
        s_ctx = jnp.einsum('bqkgd,bpkd->bkgqp', qi, ctx_k, preferred_element_type=F32) * scale
        sk = jnp.broadcast_to(sink32, s_loc.shape[:-1] + (1,))
        pr = jax.nn.softmax(jnp.concatenate([s_loc, s_ctx, sk], axis=-1), axis=-1).astype(v.dtype)
        return (jnp.einsum('bkgqs,bskd->bqkgd', pr[..., :span], vb)
                + jnp.einsum('bkgqp,bpkd->bqkgd', pr[..., span:span + P], ctx_v))

    o = lax.map(block, (jnp.arange(nb), qb))
    return jnp.moveaxis(o, 0, 1).reshape(Bn, L, Hk * G * dh)


def neighborhood_attn_latent(q, k, v, ctx_k, ctx_v, rpb):
    Bn, L, H, dh = q.shape
    rows = L // GRID_W
    kr = min(NA_ROWS, rows)
    n_nb = kr * NA_COLS
    scale = dh ** -0.5
    qg = jnp.moveaxis(q.reshape(Bn, rows, GRID_W, H, dh), 1, 0)
    kg = k.reshape(Bn, rows, GRID_W, H, dh)
    vg = v.reshape(Bn, rows, GRID_W, H, dh)
    col = jnp.arange(GRID_W)
    cstart = jnp.clip(col - NA_COLS // 2, 0, GRID_W - NA_COLS)
    col_idx = cstart[:, None] + jnp.arange(NA_COLS)[None, :]
    dcol = col_idx - col[:, None] + (NA_COLS - 1)
    rpb32 = rpb.astype(F32)

    def row_block(args):
        r, qr = args
        rstart = jnp.clip(r - NA_ROWS // 2, 0, rows - kr)
        kband = lax.dynamic_slice_in_dim(kg, rstart, kr, axis=1)
        vband = lax.dynamic_slice_in_dim(vg, rstart, kr, axis=1)
        kn = kband[:, :, col_idx]
        vn = vband[:, :, col_idx]
        drow = rstart + jnp.arange(kr) - r + (NA_ROWS - 1)
        bias = rpb32[:, drow[None, :, None], dcol[:, None, :]]
        s_nb = jnp.einsum('bwhd,brwjhd->bhwrj', qr, kn, preferred_element_type=F32) * scale + bias[None]
        s_nb = s_nb.reshape(Bn, H, GRID_W, n_nb)
        s_ctx = jnp.einsum('bwhd,bphd->bhwp', qr, ctx_k, preferred_element_type=F32) * scale
        pr = jax.nn.softmax(jnp.concatenate([s_nb, s_ctx], axis=-1), axis=-1).astype(v.dtype)
        p_nb = pr[..., :n_nb].reshape(Bn, H, GRID_W, kr, NA_COLS)
        return (jnp.einsum('bhwrj,brwjhd->bwhd', p_nb, vn)
                + jnp.einsum('bhwp,bphd->bwhd', pr[..., n_nb:], ctx_v))

    o = lax.map(row_block, (jnp.arange(rows), qg))
    return jnp.moveaxis(o, 0, 1).reshape(Bn, L, H * dh)


def hyena_spectrum(L, p):
    t = jnp.arange(L, dtype=F32) / L
    bands = 2.0 * jnp.pi * jnp.arange(1, HYENA_BANDS + 1, dtype=F32)
    ang = t[:, None] * bands[None, :]
    feats = jnp.concatenate([t[:, None], jnp.sin(ang), jnp.cos(ang)], axis=-1)
    freq = p["hy_freq"].astype(F32)
    z = jnp.sin(freq[0] * (feats @ p["hy_w1"].astype(F32) + p["hy_b1"].astype(F32)))
    z = jnp.sin(freq[1] * (z @ p["hy_w2"].astype(F32) + p["hy_b2"].astype(F32)))
    h = (z @ p["hy_w3"].astype(F32)).reshape(L, HYENA_ORDER, 2, B_WIDTH)
    h = h * jnp.exp(-jnp.abs(p["hy_decay"].astype(F32))[None] * t[:, None, None, None])
    fwd, bwd = h[:, :, 0], h[:, :, 1]
    two = jnp.concatenate([fwd, jnp.zeros((1, HYENA_ORDER, B_WIDTH), F32), bwd[:0:-1]], axis=0)
    return jnp.fft.rfft(two, axis=0)


def fft_conv(u, spec, skip):
    L = u.shape[1]
    uf = u.astype(F32)
    y = jnp.fft.irfft(jnp.fft.rfft(uf, n=2 * L, axis=1) * spec[None], n=2 * L, axis=1)[:, :L]
    return (y + uf * skip.astype(F32)).astype(u.dtype)


def short_conv(u, w, b):
    L = u.shape[1]
    half = SHORT_CONV // 2
    up = jnp.pad(u, ((0, 0), (half, SHORT_CONV - 1 - half), (0, 0)))
    y = b
    for j in range(SHORT_CONV):
        y = y + up[:, j:j + L] * w[j]
    return y


def hyena_mix(u, p):
    L = u.shape[1]
    spec = hyena_spectrum(L, p)
    u = short_conv(u, p["hy_conv_w"], p["hy_conv_b"])
    v, x1, x2 = jnp.split(u, 3, axis=-1)
    z = x1 * fft_conv(v, spec[:, 0], p["hy_skip"][0])
    return x2 * fft_conv(z, spec[:, 1], p["hy_skip"][1])


def merge_branches(ya, ag, yb, bg, yc, cg, mg, p):
    ga, gb, gc = jnp.split(mg, N_BRANCH, axis=-1)
    m = (jax.nn.sigmoid(ga) * jnp.matmul(ya * jax.nn.silu(ag), p["w_up_a"])
         + jax.nn.sigmoid(gb) * jnp.matmul(yb * jax.nn.silu(bg), p["w_up_b"])
         + jax.nn.sigmoid(gc) * jnp.matmul(yc * jax.nn.silu(cg), p["w_up_c"]))
    return jnp.matmul(m, p["w_out"])


def context_layer(x, cond, p):
    Bn, S, _ = x.shape
    (aq, ak, av, ag, bu, bg, cqkv, cg, mg), gate = project(x, cond, p)
    aq = aq.reshape(Bn, S, A_KV_HEADS, A_GROUP, HEAD_DIM)
    ak = ak.reshape(Bn, S, A_KV_HEADS, HEAD_DIM)
    av = av.reshape(Bn, S, A_KV_HEADS, HEAD_DIM)
    ya = context_attn(aq, ak, av, p["a_sink"].reshape(A_KV_HEADS, A_GROUP))
    yb = hyena_mix(bu, p)
    cq, ck, cv = jnp.split(cqkv, 3, axis=-1)
    cq = cq.reshape(Bn, S, C_HEADS, 1, HEAD_DIM)
    ck = ck.reshape(Bn, S, C_HEADS, HEAD_DIM)
    cv = cv.reshape(Bn, S, C_HEADS, HEAD_DIM)
    yc = context_attn(cq, ck, cv, None)
    out = merge_branches(ya, ag, yb, bg, yc, cg, mg, p)
    return x + gate * out, ak, av, ck, cv


def latent_layer(x, cond, ctx_ak, ctx_av, ctx_ck, ctx_cv, p):
    Bn, L, _ = x.shape
    (aq, ak, av, ag, bu, bg, cqkv, cg, mg), gate = project(x, cond, p)
    cos, sin = axial_rope(L)
    aq = apply_rope(aq.reshape(Bn, L, A_HEADS, HEAD_DIM), cos, sin).reshape(Bn, L, A_KV_HEADS, A_GROUP, HEAD_DIM)
    ak = apply_rope(ak.reshape(Bn, L, A_KV_HEADS, HEAD_DIM), cos, sin)
    av = av.reshape(Bn, L, A_KV_HEADS, HEAD_DIM)
    ya = window_attn_latent(aq, ak, av, ctx_ak, ctx_av, p["a_sink"].reshape(A_KV_HEADS, A_GROUP))
    yb = hyena_mix(bu, p)
    cq, ck, cv = jnp.split(cqkv, 3, axis=-1)
    cq = cq.reshape(Bn, L, C_HEADS, HEAD_DIM)
    ck = ck.reshape(Bn, L, C_HEADS, HEAD_DIM)
    cv = cv.reshape(Bn, L, C_HEADS, HEAD_DIM)
    yc = neighborhood_attn_latent(cq, ck, cv, ctx_ck, ctx_cv, p["c_rpb"])
    out = merge_branches(ya, ag, yb, bg, yc, cg, mg, p)
    return x + gate * out


def setup_inputs(seed: int = 0) -> dict:
    key = jax.random.key(seed)
    ks = jax.random.split(key, 32)

    def nrm(k, shape, s):
        return s * jax.random.normal(k, shape, F32)

    return {
        "x_prompt": nrm(ks[0], (BATCH, SEQ, D_MODEL), 1.0),
        "x_sample": nrm(ks[1], (DEC_BATCH, DEC_SEQ, D_MODEL), 1.0),
        "c": nrm(ks[2], (DEC_BATCH, D_MODEL), 1.0),
        "cache_a_k": nrm(ks[3], (DEC_BATCH, DEPTH, PAST_LEN, A_KV_HEADS, HEAD_DIM), 1.0),
        "cache_a_v": nrm(ks[4], (DEC_BATCH, DEPTH, PAST_LEN, A_KV_HEADS, HEAD_DIM), 1.0),
        "cache_c_k": nrm(ks[5], (DEC_BATCH, DEPTH, PAST_LEN, C_HEADS, HEAD_DIM), 1.0),
        "cache_c_v": nrm(ks[6], (DEC_BATCH, DEPTH, PAST_LEN, C_HEADS, HEAD_DIM), 1.0),
        "c_ctx": nrm(ks[7], (D_MODEL,), 1.0),
        "norm_w": 1.0 + nrm(ks[8], (DEPTH, D_MODEL), 0.01),
        "w_ada": nrm(ks[9], (DEPTH, D_MODEL, 3 * D_MODEL), 0.5 * D_MODEL ** -0.5),
        "b_ada": nrm(ks[10], (DEPTH, 3 * D_MODEL), 0.1),
        "w_in": nrm(ks[11], (DEPTH, D_MODEL, IN_WIDTH), D_MODEL ** -0.5),
        "a_sink": nrm(ks[12], (DEPTH, A_HEADS), 1.0),
        "hy_conv_w": nrm(ks[13], (DEPTH, SHORT_CONV, 3 * B_WIDTH), SHORT_CONV ** -0.5),
        "hy_conv_b": nrm(ks[14], (DEPTH, 3 * B_WIDTH), 0.02),
        "hy_w1": nrm(ks[15], (DEPTH, HYENA_EMB, HYENA_HIDDEN), HYENA_EMB ** -0.5),
        "hy_b1": nrm(ks[16], (DEPTH, HYENA_HIDDEN), 0.1),
        "hy_w2": nrm(ks[17], (DEPTH, HYENA_HIDDEN, HYENA_HIDDEN), HYENA_HIDDEN ** -0.5),
        "hy_b2": nrm(ks[18], (DEPTH, HYENA_HIDDEN), 0.1),
        "hy_freq": 1.0 + nrm(ks[19], (DEPTH, 2, HYENA_HIDDEN), 0.1),
        "hy_w3": nrm(ks[20], (DEPTH, HYENA_HIDDEN, HYENA_ORDER * 2 * B_WIDTH), 0.05 * HYENA_HIDDEN ** -0.5),
        "hy_decay": jax.random.uniform(ks[21], (DEPTH, HYENA_ORDER, 2, B_WIDTH), F32, 3.0, 15.0),
        "hy_skip": 1.0 + nrm(ks[22], (DEPTH, HYENA_ORDER, B_WIDTH), 0.1),
        "c_rpb": nrm(ks[23], (DEPTH, C_HEADS, 2 * NA_ROWS - 1, 2 * NA_COLS - 1), 0.1),
        "w_up_a": nrm(ks[24], (DEPTH, A_WIDTH, D_MODEL), A_WIDTH ** -0.5),
        "w_up_b": nrm(ks[25], (DEPTH, B_WIDTH, D_MODEL), B_WIDTH ** -0.5),
        "w_up_c": nrm(ks[26], (DEPTH, C_WIDTH, D_MODEL), C_WIDTH ** -0.5),
        "w_out": nrm(ks[27], (DEPTH, D_MODEL, D_MODEL), D_MODEL ** -0.5),
        "final_norm_w": 1.0 + nrm(ks[28], (D_MODEL,), 0.01),
    }


def reference(x_prompt, x_sample, c, cache_a_k, cache_a_v, cache_c_k, cache_c_v, c_ctx,
              norm_w, w_ada, b_ada, w_in, a_sink, hy_conv_w, hy_conv_b, hy_w1, hy_b1, hy_w2, hy_b2,
              hy_freq, hy_w3, hy_decay, hy_skip, c_rpb, w_up_a, w_up_b, w_up_c, w_out, final_norm_w):
    def layer_params(l):
        return {
            "norm_w": norm_w[l], "w_ada": w_ada[l], "b_ada": b_ada[l], "w_in": w_in[l],
            "a_sink": a_sink[l], "hy_conv_w": hy_conv_w[l], "hy_conv_b": hy_conv_b[l],
            "hy_w1": hy_w1[l], "hy_b1": hy_b1[l], "hy_w2": hy_w2[l], "hy_b2": hy_b2[l],
            "hy_freq": hy_freq[l], "hy_w3": hy_w3[l], "hy_decay": hy_decay[l], "hy_skip": hy_skip[l],
            "c_rpb": c_rpb[l], "w_up_a": w_up_a[l], "w_up_b": w_up_b[l], "w_up_c": w_up_c[l],
            "w_out": w_out[l],
        }

    cond_ctx = c_ctx[None, None, :]
    xp = x_prompt
    aks, avs, cks, cvs = [], [], [], []
    for l in range(DEPTH):
        xp, ak, av, ck, cv = context_layer(xp, cond_ctx, layer_params(l))
        aks.append(ak)
        avs.append(av)
        cks.append(ck)
        cvs.append(cv)
    y_prompt = rms_norm(xp, final_norm_w)
    new_a_k = jnp.stack(aks, axis=1)
    new_a_v = jnp.stack(avs, axis=1)
    new_c_k = jnp.stack(cks, axis=1)
    new_c_v = jnp.stack(cvs, axis=1)

    cond_lat = c[:, None, :]
    xs = x_sample
    for l in range(DEPTH):
        xs = latent_layer(xs, cond_lat, cache_a_k[:, l], cache_a_v[:, l], cache_c_k[:, l], cache_c_v[:, l],
                          layer_params(l))
    y_sample = rms_norm(xs, final_norm_w)

    return (y_prompt, y_sample, new_a_k, new_a_v, new_c_k, new_c_v)
```

```python
import types
import numpy as np
import ml_dtypes
from contextlib import ExitStack
import concourse.bass as bass
import concourse.mybir as mybir
from concourse.bass_utils import run_bass_kernel_spmd

F32 = mybir.dt.float32
BF16 = mybir.dt.bfloat16
AF = mybir.ActivationFunctionType
ALU = mybir.AluOpType
ENGS = ["pe", "act", "dve", "pool", "sp"]

D = 2048
NS = 4096
NP = 1024
NT = NS + NP
INW = 13312
NEG = -30000.0
EPS = 1e-6


class Res:
    __slots__ = ("last_w", "readers")

    def __init__(self):
        self.last_w = None
        self.readers = []


def _freeze(fn):
    if getattr(fn, "__closure__", None) is None:
        return fn
    cells = []
    for c in fn.__closure__:
        try:
            cells.append(types.CellType(c.cell_contents))
        except ValueError:
            cells.append(c)
    g = types.FunctionType(fn.__code__, fn.__globals__, fn.__name__, fn.__defaults__, tuple(cells))
    g.__kwdefaults__ = fn.__kwdefaults__
    return g


class Prog:
    def __init__(self, nc):
        self.nc = nc
        self.stack = ExitStack()
        self.eng = {"pe": nc.tensor, "act": nc.scalar, "dve": nc.vector, "pool": nc.gpsimd, "sp": nc.sync}
        self.sem = {e: self.stack.enter_context(nc.semaphore("c_" + e)) for e in ENGS}
        nd = {"sp": 28, "pool": 20, "act": 4}
        self.dsem = {q: [self.stack.enter_context(nc.semaphore(f"d_{q}{i}")) for i in range(n)] for q, n in nd.items()}
        self.dcnt = {q: 0 for q in self.dsem}
        self.dval = {q: [0] * len(v) for q, v in self.dsem.items()}
        self.cnt = {e: 0 for e in ENGS}
        self.q = {e: [] for e in ENGS}
        self.waited = {}
        self.pending_dma = []
        self.ninst = 0

    def _need(self, eng, tok, waits):
        if tok is None:
            return
        if tok[0] == "e":
            _, e2, idx = tok
            if e2 == eng and eng == "pe":
                return
            key = (eng, "e", e2)
            if self.waited.get(key, 0) >= idx:
                return
            self.waited[key] = idx
            waits.append((self.sem[e2], idx))
        else:
            _, qn, si, target = tok
            key = (eng, "d", qn, si)
            if self.waited.get(key, 0) >= target:
                return
            self.waited[key] = target
            waits.append((self.dsem[qn][si], target))

    def _deps(self, eng, r, w):
        waits = []
        for res in r:
            self._need(eng, res.last_w, waits)
        for res in w:
            self._need(eng, res.last_w, waits)
            for t in res.readers:
                self._need(eng, t, waits)
        return waits

    def op(self, eng, fns, r=(), w=()):
        if not isinstance(fns, (list, tuple)):
            fns = [fns]
        r = [x.r if hasattr(x, "r") else x for x in r]
        w = [x.r if hasattr(x, "r") else x for x in w]
        waits = self._deps(eng, r, w)
        self.cnt[eng] += 1
        tok = ("e", eng, self.cnt[eng])
        self.q[eng].append((waits, [_freeze(f) for f in fns], (self.sem[eng], 1)))
        for res in r:
            res.readers.append(tok)
        for res in w:
            res.last_w = tok
            res.readers = []
        self.ninst += len(fns)
        return tok

    def dma(self, qn, out, in_, r=(), w=(), **kw):
        r = [x.r if hasattr(x, "r") else x for x in r]
        w = [x.r if hasattr(x, "r") else x for x in w]
        waits = self._deps(qn, r, w)
        n = self.dcnt[qn]
        self.dcnt[qn] += 1
        si = n % len(self.dsem[qn])
        prev = self.dval[qn][si]
        if prev > 0:
            key = (qn, "d", qn, si)
            if self.waited.get(key, 0) < prev:
                self.waited[key] = prev
                waits.append((self.dsem[qn][si], prev))
        target = prev + 16
        self.dval[qn][si] = target
        tok = ("d", qn, si, target)
        self.q[qn].append((waits, [lambda e, o=out, i=in_, k=kw: e.dma_start(out=o, in_=i, **k)], (self.dsem[qn][si], 16)))
        for res in r:
            res.readers.append(tok)
        for res in w:
            res.last_w = tok
            res.readers = []
        self.pending_dma.append(tok)
        self.ninst += 1
        return tok

    def barrier(self):
        toks = [("e", e, self.cnt[e]) for e in ENGS if self.cnt[e] > 0]
        best = {}
        for t in self.pending_dma:
            k = (t[1], t[2])
            if k not in best or best[k][3] < t[3]:
                best[k] = t
        toks += list(best.values())
        for eng in ENGS:
            waits = []
            for t in toks:
                if t[0] == "e" and t[1] == eng:
                    continue
                self._need(eng, t, waits)
            if waits:
                self.q[eng].append((waits, [], None))
        self.pending_dma = []

    def emit(self):
        nc = self.nc
        qs = self.q
        self.q = {e: [] for e in ENGS}

        def run(engname):
            def f(e):
                for waits, fns, inc in qs[engname]:
                    for s, v in waits:
                        e.wait_ge(s, v)
                    last = None
                    for fn in fns:
                        last = fn(e)
                    if inc is not None and last is not None:
                        last.then_inc(inc[0], inc[1])
            return f

        with nc.Block() as block:
            block.tensor(run("pe"))
            block.scalar(run("act"))
            block.vector(run("dve"))
            block.gpsimd(run("pool"))
            block.sync(run("sp"))

    def close(self):
        self.stack.close()


class Tl:
    __slots__ = ("h", "r")

    def __init__(self, h):
        self.h = h
        self.r = Res()


_uid = [0]


def _name(p):
    _uid[0] += 1
    return f"{p}{_uid[0]}"


def _bf(a):
    return np.ascontiguousarray(a.astype(ml_dtypes.bfloat16))


def _dft_tables(L):
    N = 2 * L
    n = L // 128
    t = np.arange(L, dtype=np.int64)
    m = (t[:, None] * t[None, :]) % N
    ang = (2.0 * np.pi / N) * m.astype(np.float64)
    C = np.cos(ang)
    S = np.sin(ang)

    def tile(M):
        return _bf(M.reshape(n, 128, n, 128).transpose(2, 1, 0, 3))
    return tile(C), tile(S)


def _consts():
    cs = {}
    cs["ident"] = _bf(np.eye(128))
    cs["identf"] = np.eye(128, dtype=np.float32)
    pm = np.zeros((128, 128))
    for m in range(128):
        k = m + 32 if (m % 64) < 32 else m - 32
        pm[k, m] = 1.0
    cs["perm"] = _bf(pm)
    t = np.arange(NS)
    row = (t // 64).astype(np.float64)
    col = (t % 64).astype(np.float64)
    nf = 16
    inv = np.power(10000.0, -np.arange(nf, dtype=np.float64) / nf)
    ang = np.concatenate([row[None, :] * inv[:, None], col[None, :] * inv[:, None]], axis=0)
    cosT = np.cos(ang)
    sinT = np.sin(ang)
    rc = np.zeros((128, NS))
    rs = np.zeros((128, NS))
    for p in range(128):
        rc[p] = cosT[p % 32]
        rs[p] = -sinT[p % 32] if (p % 64) < 32 else sinT[p % 32]
    cs["ropeC"] = rc.astype(np.float32)
    cs["ropeS"] = rs.astype(np.float32)
    k = np.arange(128)[:, None]
    q = np.arange(128)[None, :]
    tri = np.zeros((128, 2, 3, 128), np.float32)
    tri[:, 0] = np.where(q <= k, 0.0, NEG)[:, None, :]
    tri[:, 1] = np.where(k <= q, 0.0, NEG)[:, None, :]
    cs["tri"] = tri.reshape(128, 2, 384)
    for L, tag in ((NS, "s"), (256, "p")):
        C, S = _dft_tables(L)
        cs["ctab_" + tag] = C
        cs["stab_" + tag] = S
        n = L // 128
        tt = np.arange(L)
        alt = np.where(tt % 2 == 0, 1.0, -1.0)
        cs["altc_" + tag] = _bf(alt.reshape(n, 128).T)
        cs["altr_" + tag] = _bf(alt.reshape(1, L))
        tf = (np.arange(L, dtype=np.float32) / np.float32(L)).astype(np.float32)
        bands = (2.0 * np.pi * np.arange(1, 17, dtype=np.float32)).astype(np.float32)
        a = tf[:, None] * bands[None, :]
        feats = np.concatenate([tf[:, None], np.sin(a), np.cos(a)], axis=-1).astype(np.float32)
        cs["feats_" + tag] = np.ascontiguousarray(feats.T)
        cs["negt_" + tag] = np.ascontiguousarray((-tf).reshape(n, 128).T)
        sc = np.full((128, 1), 2.0 / (2 * L), np.float32)
        sc[0, 0] = 1.0 / (2 * L)
        cs["sc0_" + tag] = sc
    return cs


def _na_tables(rpb):
    def rstart(r):
        return min(max(r - 4, 0), 56)

    def cstart(w):
        return min(max(w - 8, 0), 48)

    def table(j, lo, nch):
        p = np.arange(128)
        q = np.arange(128)
        out = np.full((128, 12, nch, 128), NEG, np.float32)
        r = 2 * j + q // 64
        w = q % 64
        rs = np.array([rstart(x) for x in r])
        cst = np.array([cstart(x) for x in w])
        for ci in range(nch):
            c = lo + ci
            kr = 2 * c + p // 64
            col = p % 64
            valid = ((kr[:, None] >= rs[None, :]) & (kr[:, None] < rs[None, :] + 8)
                     & (col[:, None] >= cst[None, :]) & (col[:, None] < cst[None, :] + 16))
            drow = np.clip(kr[:, None] - r[None, :] + 7, 0, 14)
            dcol = np.clip(col[:, None] - w[None, :] + 15, 0, 30)
            g = rpb[:, drow, dcol]
            out[:, :, ci, :] = np.where(valid[:, None, :], g.transpose(1, 0, 2), NEG)
        return out
    interior = table(2, 0, 5)
    edge = np.stack([table(0, 0, 4), table(1, 0, 4), table(30, 28, 4), table(31, 28, 4)], axis=0)
    return interior, edge


DEBUG = {}
SEGS = [("aq", 0, 3), ("ak", 3, 4), ("av", 4, 5), ("ag", 5, 8), ("bu", 8, 14), ("bg", 14, 16),
        ("cq", 16, 19), ("ck", 19, 22), ("cv", 22, 25), ("cg", 25, 28), ("mg", 28, 52)]


def seg_of(wb):
    for nm, a, b in SEGS:
        if a <= wb < b:
            return nm, wb - a
    raise ValueError


def build(dbg=False, layers=(0, 1), phases=("prep", "filt", "proj", "attn", "hy", "merge")):
    nc = bass.Bass("TRN2", target_bir_lowering=False)
    P = Prog(nc)

    def din(name, shape, dt=F32):
        return nc.dram_tensor(name, list(shape), dt, kind="ExternalInput").ap()

    def dout(name, shape, dt=F32):
        return nc.dram_tensor(name, list(shape), dt, kind="ExternalOutput").ap()

    def dscr(name, shape, dt=BF16):
        k = "ExternalOutput" if (dbg and name in DBG_NAMES) else "Internal"
        return nc.dram_tensor(name, list(shape), dt, kind=k).ap()

    xin = din("xin", [NT, D])
    c_lat = din("c_lat", [D])
    c_ctx = din("c_ctx", [D])
    cak = din("cak", [2, 512, 256]); cav = din("cav", [2, 512, 256])
    cck = din("cck", [2, 512, 768]); ccv = din("ccv", [2, 512, 768])
    norm_w = din("norm_w", [2, D]); w_ada = din("w_ada", [2, D, 3 * D]); b_ada = din("b_ada", [2, 3 * D])
    w_in = din("w_in", [2, D, INW]); a_sink = din("a_sink", [2, 12])
    hy_conv_w = din("hy_conv_w", [2, 3, 1536]); hy_conv_b = din("hy_conv_b", [2, 1536])
    hy_w1 = din("hy_w1", [2, 33, 64]); hy_b1 = din("hy_b1", [2, 64]); hy_w2 = din("hy_w2", [2, 64, 64])
    hy_b2 = din("hy_b2", [2, 64]); hy_freq = din("hy_freq", [2, 2, 64]); hy_w3 = din("hy_w3", [2, 64, 2048])
    hy_decay = din("hy_decay", [2, 2048]); hy_skip = din("hy_skip", [2, 2, 512])
    nab_i = din("nab_i", [2, 128, 12 * 5 * 128]); nab_e = din("nab_e", [2, 4, 128, 12 * 4 * 128])
    w_up_a = din("w_up_a", [2, 768, D]); w_up_b = din("w_up_b", [2, 512, D]); w_up_c = din("w_up_c", [2, 768, D])
    w_out = din("w_out", [2, D, D]); final_norm_w = din("final_norm_w", [D])
    ident_d = din("ident", [128, 128], BF16); identf_d = din("identf", [128, 128]); perm_d = din("perm", [128, 128], BF16)
    ropeC_d = din("ropeC", [128, NS]); ropeS_d = din("ropeS", [128, NS]); tri_d = din("tri", [128, 2, 384])
    tabs = {}
    for tag, L in (("s", NS), ("p", 256)):
        n = L // 128
        tabs[tag] = dict(L=L, n=n,
                         ctab=din("ctab_" + tag, [n, 128, n, 128], BF16), stab=din("stab_" + tag, [n, 128, n, 128], BF16),
                         altc=din("altc_" + tag, [128, n], BF16), altr=din("altr_" + tag, [1, L], BF16),
                         feats=din("feats_" + tag, [33, L]), negt=din("negt_" + tag, [128, n]), sc0=din("sc0_" + tag, [128, 1]))
    y_out = dout("y", [NT, D])
    nak = dout("nak", [4 * 2 * 256, 256]); nav = dout("nav", [4 * 2 * 256, 256])
    nck = dout("nck", [4 * 2 * 256, 768]); ncv = dout("ncv", [4 * 2 * 256, 768])
    DBG_NAMES = set(dbg) if dbg else set()
    w_in_b = dscr("w_in_b", [2, D, INW]); w_upa_b = dscr("w_upa_b", [2, 768, D]); w_upb_b = dscr("w_upb_b", [2, 512, D])
    w_upc_b = dscr("w_upc_b", [2, 768, D]); w_out_b = dscr("w_out_b", [2, D, D])
    modv = dscr("modv", [2, 2, 3, D], F32)
    qaT = dscr("qaT", [768, NT]); kaT = dscr("kaT", [256, NT]); va = dscr("va", [NT, 260]); ag = dscr("ag", [NT, 768])
    bu = dscr("bu", [NT, 1536]); bg = dscr("bg", [NT, 512]); qcT = dscr("qcT", [768, NT]); kcT = dscr("kcT", [768, NT])
    vc = dscr("vc", [NT, 780]); cg = dscr("cg", [NT, 768]); mgT = dscr("mgT", [6144, NT])
    yaT = dscr("yaT", [768, NT]); ybT = dscr("ybT", [512, NT]); ycT = dscr("ycT", [768, NT])
    x1c = dscr("x1c", [NT, 512]); x2c = dscr("x2c", [NT, 512])
    xres = dscr("xres", [NT, D], F32)
    spec = {tag: dscr("spec_" + tag, [2, 2, 2, tabs[tag]["L"], 512], F32) for tag in ("s", "p")}
    specn = {tag: dscr("specn_" + tag, [2, 2, 512], F32) for tag in ("s", "p")}
    R = {k: Res() for k in ["wb", "modv", "proj", "mix", "xres", "spec", "x12"]}

    def tile(st, shape, dt, pfx="t"):
        return Tl(st.enter_context(nc.sbuf_tensor(_name(pfx), list(shape), dt)))

    def ptile(st, shape, dt=F32, pfx="p"):
        return Tl(st.enter_context(nc.psum_tensor(_name(pfx), list(shape), dt)))

    def finish():
        P.barrier()
        P.emit()

    def phase_cast():
        for l in layers:
            for (src, dst, rows, cols) in ((w_in, w_in_b, D, INW), (w_up_a, w_upa_b, 768, D), (w_up_b, w_upb_b, 512, D),
                                           (w_up_c, w_upc_b, 768, D), (w_out, w_out_b, D, D)):
                step = 128 if cols > 4096 else 256
                for r0 in range(0, rows, step):
                    P.dma("pool", dst[l, r0:r0 + step, :], src[l, r0:r0 + step, :], w=[R["wb"]], max_dma_last_dim=4096)

    def phase_mod():
        with ExitStack() as st:
            idf = tile(st, [128, 128], F32)
            cc = tile(st, [16, 2, 128], F32)
            scT = tile(st, [128, 2, 16], F32)
            pT = ptile(st, [128, 2, 16], F32)
            wt = [tile(st, [128, 16, 512], F32) for _ in range(2)]
            pm = [ptile(st, [128, 512]) for _ in range(2)]
            bb = tile(st, [2, 3 * D], F32)
            nw2 = tile(st, [2, D], F32)
            mod = tile(st, [2, 3 * D], F32)
            P.dma("sp", idf.h[:], identf_d, w=[idf])
            P.dma("sp", cc.h[:, 0, :], c_ctx.rearrange("(k p) -> k p", p=128), w=[cc])
            P.dma("sp", cc.h[:, 1, :], c_lat.rearrange("(k p) -> k p", p=128), w=[cc])
            P.op("pe", [lambda e, g=g: e.transpose(out=pT.h[:, g, :], in_=cc.h[:, g, :], identity=idf.h[0:16, 0:16]) for g in range(2)],
                 r=[cc, idf], w=[pT])
            P.op("act", lambda e: e.activation(out=scT.h[:], in_=pT.h[:], func=AF.Silu), r=[pT], w=[scT])
            for l in layers:
                P.dma("sp", bb.h[:], b_ada[l].partition_broadcast(2), w=[bb])
                P.dma("sp", nw2.h[:], norm_w[l].partition_broadcast(2), w=[nw2])
                for cb in range(12):
                    w_t = wt[cb % 2]
                    p_t = pm[cb % 2]
                    P.dma("sp", w_t.h[:], w_ada[l, :, cb * 512:(cb + 1) * 512].rearrange("(kc p) n -> p kc n", p=128), w=[w_t])
                    P.op("pe", [lambda e, kc=kc, w_t=w_t, p_t=p_t: e.matmul(p_t.h[0:2, :], lhsT=scT.h[:, :, kc], rhs=w_t.h[:, kc, :],
                                                                           start=(kc == 0), stop=(kc == 15)) for kc in range(16)],
                         r=[scT, w_t], w=[p_t])
                    P.op("dve", lambda e, cb=cb, p_t=p_t: e.tensor_tensor(out=mod.h[:, cb * 512:(cb + 1) * 512], in0=p_t.h[0:2, :],
                                                                         in1=bb.h[:, cb * 512:(cb + 1) * 512], op=ALU.add),
                         r=[p_t, bb], w=[mod])
                P.op("dve", lambda e: e.scalar_tensor_tensor(out=mod.h[:, D:2 * D], in0=mod.h[:, D:2 * D], scalar=1.0, in1=nw2.h[:],
                                                             op0=ALU.add, op1=ALU.mult), r=[mod, nw2], w=[mod])
                for g in range(2):
                    for j, (a, b) in enumerate(((D, 2 * D), (0, D), (2 * D, 3 * D))):
                        P.dma("sp", modv[l, g, j, :].rearrange("(o d) -> o d", o=1), mod.h[g:g + 1, a:b], r=[mod], w=[R["modv"]])
        finish()

    def load_tab(tb, oc, ct, stt):
        P.dma("sp", ct.h[:], tb["ctab"][oc], w=[ct])
        P.dma("sp", stt.h[:], tb["stab"][oc], w=[stt])

    def phase_filt(tag):
        tb = tabs[tag]
        L, n = tb["L"], tb["n"]
        N = 2 * L
        nch = max(1, L // 512)
        cw = min(L, 512)
        TWO_PI = float(2 * np.pi)
        with ExitStack() as st:
            feats = tile(st, [33, L], F32)
            negt = tile(st, [128, n], F32)
            sc0 = tile(st, [128, 1], F32)
            altc = tile(st, [128, n], BF16)
            z1 = tile(st, [64, L], F32)
            z2 = tile(st, [64, L], F32)
            xa = tile(st, [64, 512], F32)
            mk = tile(st, [64, 512], F32)
            w1 = tile(st, [33, 64], F32); w2 = tile(st, [64, 64], F32); w3 = tile(st, [64, 2048], F32)
            fr = tile(st, [64, 2], F32); b1 = tile(st, [64, 1], F32); b2 = tile(st, [64, 1], F32)
            fb = tile(st, [64, 2], F32)
            dec = tile(st, [128, 1024], F32)
            E = tile(st, [128, 1024], F32)
            hf = tile(st, [128, 1024], F32)
            fbp = tile(st, [128, n, 512], BF16); fbm = tile(st, [128, n, 512], BF16)
            ct = [tile(st, [128, n, 128], BF16) for _ in range(2)]
            stt = [tile(st, [128, n, 128], BF16) for _ in range(2)]
            osr = [tile(st, [128, 512], F32) for _ in range(2)]
            osi = [tile(st, [128, 512], F32) for _ in range(2)]
            osn = tile(st, [1, 512], F32)
            pz = [ptile(st, [128, 512]) for _ in range(2)]
            ph = [ptile(st, [128, 512]) for _ in range(2)]
            pr = [ptile(st, [128, 512]) for _ in range(2)]
            pi = [ptile(st, [128, 512]) for _ in range(2)]
            P.dma("sp", feats.h[:], tb["feats"], w=[feats])
            P.dma("sp", negt.h[:], tb["negt"], w=[negt])
            P.dma("sp", sc0.h[:], tb["sc0"], w=[sc0])
            P.dma("sp", altc.h[:], tb["altc"], w=[altc])
            for l in layers:
                P.dma("sp", w1.h[:], hy_w1[l], w=[w1]); P.dma("sp", w2.h[:], hy_w2[l], w=[w2]); P.dma("sp", w3.h[:], hy_w3[l], w=[w3])
                for o in range(2):
                    P.dma("sp", fr.h[:, o:o + 1], hy_freq[l, o].rearrange("(p o) -> p o", o=1), w=[fr])
                P.dma("sp", b1.h[:], hy_b1[l].rearrange("(p o) -> p o", o=1), w=[b1])
                P.dma("sp", b2.h[:], hy_b2[l].rearrange("(p o) -> p o", o=1), w=[b2])
                P.op("dve", lambda e: e.tensor_tensor(out=fb.h[:, 0:1], in0=fr.h[:, 0:1], in1=b1.h[:], op=ALU.mult), r=[fr, b1], w=[fb])
                P.op("dve", lambda e: e.tensor_tensor(out=fb.h[:, 1:2], in0=fr.h[:, 1:2], in1=b2.h[:], op=ALU.mult), r=[fr, b2], w=[fb])
                for li, (wl, K, src, dst) in enumerate(((w1, 33, feats, z1), (w2, 64, z1, z2))):
                    for ch in range(nch):
                        p_t = pz[ch % 2]
                        P.op("pe", lambda e, p_t=p_t, wl=wl, K=K, src=src, ch=ch: e.matmul(p_t.h[0:64, 0:cw], lhsT=wl.h[0:K, :],
                                                                                            rhs=src.h[0:K, ch * cw:(ch + 1) * cw], start=True, stop=True),
                             r=[wl, src], w=[p_t])
                        P.op("dve", lambda e, p_t=p_t, li=li: e.tensor_scalar(out=xa.h[:, 0:cw], in0=p_t.h[0:64, 0:cw], scalar1=fr.h[:, li:li + 1],
                                                                              scalar2=fb.h[:, li:li + 1], op0=ALU.mult, op1=ALU.add),
                             r=[p_t, fr, fb], w=[xa])
                        for (thr, cmp_, add) in ((float(np.pi), ALU.is_gt, -TWO_PI), (-float(np.pi), ALU.is_lt, TWO_PI)) * 2:
                            P.op("dve", lambda e, thr=thr, cmp_=cmp_: e.tensor_single_scalar(out=mk.h[:, 0:cw], in_=xa.h[:, 0:cw], scalar=thr, op=cmp_),
                                 r=[xa], w=[mk])
                            P.op("dve", lambda e, add=add: e.scalar_tensor_tensor(out=xa.h[:, 0:cw], in0=mk.h[:, 0:cw], scalar=add, in1=xa.h[:, 0:cw],
                                                                                  op0=ALU.mult, op1=ALU.add), r=[mk, xa], w=[xa])
                        P.op("act", lambda e, dst=dst, ch=ch: e.activation(out=dst.h[:, ch * cw:(ch + 1) * cw], in_=xa.h[:, 0:cw], func=AF.Sin),
                             r=[xa], w=[dst])
                for o in range(2):
                    P.dma("sp", dec.h[:], hy_decay[l, o * 1024:(o + 1) * 1024].partition_broadcast(128), w=[dec])
                    P.op("act", lambda e: e.activation(out=dec.h[:], in_=dec.h[:], func=AF.Abs), r=[dec], w=[dec])
                    for i in range(n):
                        P.op("act", lambda e, i=i: e.activation(out=E.h[:], in_=dec.h[:], func=AF.Exp, scale=negt.h[:, i:i + 1]),
                             r=[dec, negt], w=[E])
                        for dr in range(2):
                            p_t = ph[dr]
                            P.op("pe", lambda e, p_t=p_t, i=i, dr=dr, o=o: e.matmul(p_t.h[:], lhsT=z2.h[0:64, i * 128:(i + 1) * 128],
                                                                                   rhs=w3.h[0:64, o * 1024 + dr * 512:o * 1024 + (dr + 1) * 512],
                                                                                   start=True, stop=True), r=[z2, w3], w=[p_t])
                            P.op("dve", lambda e, p_t=p_t, dr=dr: e.tensor_tensor(out=hf.h[:, dr * 512:(dr + 1) * 512], in0=p_t.h[:],
                                                                                 in1=E.h[:, dr * 512:(dr + 1) * 512], op=ALU.mult),
                                 r=[p_t, E], w=[hf])
                        P.op("dve", lambda e, i=i: e.tensor_tensor(out=fbp.h[:, i, :], in0=hf.h[:, 0:512], in1=hf.h[:, 512:1024], op=ALU.add),
                             r=[hf], w=[fbp])
                        P.op("dve", lambda e, i=i: e.tensor_tensor(out=fbm.h[:, i, :], in0=hf.h[:, 512:1024], in1=hf.h[:, 0:512], op=ALU.subtract),
                             r=[hf], w=[fbm])
                        if i == 0:
                            P.op("dve", lambda e: e.tensor_copy(out=fbp.h[0:1, 0, :], in_=hf.h[0:1, 0:512]), r=[hf], w=[fbp])
                    for j in range(n):
                        c_t, s_t = ct[j % 2], stt[j % 2]
                        load_tab(tb, j, c_t, s_t)
                        p_r, p_i = pr[j % 2], pi[j % 2]
                        P.op("pe", [lambda e, cc=cc, p_r=p_r, c_t=c_t: e.matmul(p_r.h[:], lhsT=c_t.h[:, cc, :], rhs=fbp.h[:, cc, :],
                                                                             start=(cc == 0), stop=(cc == n - 1)) for cc in range(n)],
                             r=[c_t, fbp], w=[p_r])
                        P.op("pe", [lambda e, cc=cc, p_i=p_i, s_t=s_t: e.matmul(p_i.h[:], lhsT=s_t.h[:, cc, :], rhs=fbm.h[:, cc, :],
                                                                             start=(cc == 0), stop=(cc == n - 1)) for cc in range(n)],
                             r=[s_t, fbm], w=[p_i])
                        o_r, o_i = osr[j % 2], osi[j % 2]
                        if j == 0:
                            P.op("act", lambda e, o_r=o_r, p_r=p_r: e.activation(out=o_r.h[:], in_=p_r.h[:], func=AF.Copy, scale=sc0.h[:, 0:1]),
                                 r=[p_r, sc0], w=[o_r])
                        else:
                            P.op("act", lambda e, o_r=o_r, p_r=p_r: e.activation(out=o_r.h[:], in_=p_r.h[:], func=AF.Copy, scale=2.0 / N),
                                 r=[p_r], w=[o_r])
                        P.op("act", lambda e, o_i=o_i, p_i=p_i: e.activation(out=o_i.h[:], in_=p_i.h[:], func=AF.Copy, scale=2.0 / N),
                             r=[p_i], w=[o_i])
                        P.dma("pool", spec[tag][l, o, 0, j * 128:(j + 1) * 128, :], o_r.h[:], r=[o_r], w=[R["spec"]])
                        P.dma("pool", spec[tag][l, o, 1, j * 128:(j + 1) * 128, :], o_i.h[:], r=[o_i], w=[R["spec"]])
                    p_r = pr[0]
                    P.op("pe", [lambda e, cc=cc, p_r=p_r: e.matmul(p_r.h[0:1, :], lhsT=altc.h[:, cc:cc + 1], rhs=fbp.h[:, cc, :],
                                                                 start=(cc == 0), stop=(cc == n - 1)) for cc in range(n)],
                         r=[altc, fbp], w=[p_r])
                    P.op("act", lambda e, p_r=p_r: e.activation(out=osn.h[:], in_=p_r.h[0:1, :], func=AF.Copy, scale=1.0 / N), r=[p_r], w=[osn])
                    P.dma("pool", specn[tag][l, o, :].rearrange("(o d) -> o d", o=1), osn.h[:], r=[osn], w=[R["spec"]])
        finish()

    def phase_proj(l):
        xsrc = xin if l == 0 else xres
        with ExitStack() as st:
            idb = tile(st, [128, 128], BF16); perm = tile(st, [128, 128], BF16)
            nw = tile(st, [128, D], F32); sh = tile(st, [128, D], F32)
            xt = [tile(st, [128, D], F32) for _ in range(2)]
            junk = tile(st, [128, D], BF16)
            ss = [tile(st, [128, 1], F32) for _ in range(2)]
            tmp = tile(st, [128, D], F32)
            hb = [tile(st, [128, D], BF16) for _ in range(2)]
            hT = [tile(st, [128, 16, 512], BF16) for _ in range(2)]
            wt = [tile(st, [128, 16, 1024], BF16) for _ in range(2)]
            stF = [tile(st, [128, 2, 512], BF16) for _ in range(2)]
            stT = [tile(st, [128, 4, 256], BF16) for _ in range(2)]
            stV = [tile(st, [128, 4, 4, 65], BF16) for _ in range(2)]
            stO = [tile(st, [128, 4, 256], F32) for _ in range(2)]
            ub = tile(st, [128, 512], BF16); t1 = tile(st, [128, 512], F32); t2 = tile(st, [128, 512], F32)
            rC = tile(st, [128, 512], F32); rS = tile(st, [128, 512], F32)
            pm = [ptile(st, [128, 512]) for _ in range(4)]
            pT = ptile(st, [128, 16, 128], BF16)
            psw = ptile(st, [128, 512])
            P.dma("sp", idb.h[:], ident_d, w=[idb]); P.dma("sp", perm.h[:], perm_d, w=[perm])
            for v in stV:
                P.op("pool", lambda e, v=v: e.memset(v.h[:], 1.0), w=[v])
            cntF = cntT = cntV = cntO = 0
            pmi = 0
            nsub = 0
            nsubc = [0]

            def norm_stage(tt):
                g = 0 if tt < 8 else 1
                if tt in (0, 8):
                    P.dma("sp", nw.h[:], modv[l, 1 - g, 0, :].partition_broadcast(128), r=[R["modv"]], w=[nw])
                    P.dma("sp", sh.h[:], modv[l, 1 - g, 1, :].partition_broadcast(128), r=[R["modv"]], w=[sh])
                hTt = hT[tt % 2]
                for s in range(4):
                    nsub = nsubc[0]
                    tok0 = tt * 512 + s * 128
                    x_t = xt[nsub % 2]; s_t = ss[nsub % 2]; h_t = hb[nsub % 2]
                    nsubc[0] += 1
                    P.dma("sp", x_t.h[:], xsrc[tok0:tok0 + 128, :], r=[R["xres"]], w=[x_t])
                    P.op("act", lambda e, x_t=x_t, s_t=s_t: e.activation(out=junk.h[:], in_=x_t.h[:], func=AF.Square, accum_out=s_t.h[:]),
                         r=[x_t], w=[junk, s_t])
                    P.op("act", lambda e, s_t=s_t: e.activation(out=s_t.h[:], in_=s_t.h[:], func=AF.Sqrt, scale=1.0 / D, bias=EPS), r=[s_t], w=[s_t])
                    P.op("dve", lambda e, s_t=s_t: e.reciprocal(out=s_t.h[:], in_=s_t.h[:]), r=[s_t], w=[s_t])
                    P.op("dve", lambda e, x_t=x_t, s_t=s_t: e.scalar_tensor_tensor(out=tmp.h[:], in0=x_t.h[:], scalar=s_t.h[:, 0:1], in1=nw.h[:],
                                                                                   op0=ALU.mult, op1=ALU.mult), r=[x_t, s_t, nw], w=[tmp])
                    P.op("dve", lambda e, h_t=h_t: e.tensor_tensor(out=h_t.h[:], in0=tmp.h[:], in1=sh.h[:], op=ALU.add), r=[tmp, sh], w=[h_t])
                    P.op("pe", [lambda e, k=k, h_t=h_t: e.transpose(out=pT.h[:, k, :], in_=h_t.h[:, k * 128:(k + 1) * 128], identity=idb.h[:])
                                for k in range(16)], r=[h_t, idb], w=[pT])
                    P.op("act", lambda e, s=s, hTt=hTt: e.activation(out=hTt.h[:, :, s * 128:(s + 1) * 128], in_=pT.h[:], func=AF.Copy),
                         r=[pT], w=[hTt])

            tts = list(DEBUG.get('tts', range(NT // 512)))
            norm_stage(tts[0])
            for ti, tt in enumerate(tts):
                g = 0 if tt < 8 else 1
                sample = (g == 0)
                if sample:
                    P.dma("sp", rC.h[:], ropeC_d[:, tt * 512:(tt + 1) * 512], w=[rC])
                    P.dma("sp", rS.h[:], ropeS_d[:, tt * 512:(tt + 1) * 512], w=[rS])
                hTt = hT[tt % 2]
                for wb in DEBUG.get('wbs', range(52)):
                    if wb == 24 and ti + 1 < len(tts):
                        norm_stage(tts[ti + 1])
                    seg, bi = seg_of(wb)
                    w_t = wt[(wb // 4) % 2]
                    wo_ = (wb % 4) * 256
                    if wb % 4 == 0 or wb == DEBUG.get('wbs', [0])[0]:
                        wb0 = (wb // 4) * 1024
                        P.dma("sp", w_t.h[:], w_in_b[l, :, wb0:wb0 + 1024].rearrange("(kc p) n -> p kc n", p=128), r=[R["wb"]], w=[w_t])
                    isF = seg in ("aq", "ak", "cq", "ck", "mg")
                    if isF:
                        s_t = stF[cntF % 2]; cntF += 1
                        for half in range(2):
                            p_t = pm[pmi % 4]; pmi += 1
                            P.op("pe", [lambda e, kc=kc, half=half, p_t=p_t, w_t=w_t, wo_=wo_: e.matmul(p_t.h[:], lhsT=w_t.h[:, kc, wo_ + half * 128:wo_ + (half + 1) * 128],
                                                                                             rhs=hTt.h[:, kc, :], start=(kc == 0), stop=(kc == 15))
                                        for kc in range(16)], r=[w_t, hTt], w=[p_t])
                            if seg == "mg":
                                P.op("act", lambda e, half=half, p_t=p_t, s_t=s_t: e.activation(out=s_t.h[:, half, :], in_=p_t.h[:], func=AF.Sigmoid),
                                     r=[p_t], w=[s_t])
                            elif sample and seg in ("aq", "ak"):
                                sc = 0.125 if seg == "aq" else 1.0
                                P.op("act", lambda e, p_t=p_t, sc=sc: e.activation(out=ub.h[:], in_=p_t.h[:], func=AF.Copy, scale=sc), r=[p_t], w=[ub])
                                P.op("pe", lambda e: e.matmul(psw.h[:], lhsT=perm.h[:], rhs=ub.h[:], start=True, stop=True), r=[perm, ub], w=[psw])
                                P.op("dve", lambda e: e.tensor_tensor(out=t1.h[:], in0=ub.h[:], in1=rC.h[:], op=ALU.mult), r=[ub, rC], w=[t1])
                                P.op("dve", lambda e: e.tensor_tensor(out=t2.h[:], in0=psw.h[:], in1=rS.h[:], op=ALU.mult), r=[psw, rS], w=[t2])
                                P.op("dve", lambda e, half=half, s_t=s_t: e.tensor_tensor(out=s_t.h[:, half, :], in0=t1.h[:], in1=t2.h[:], op=ALU.add),
                                     r=[t1, t2], w=[s_t])
                            else:
                                sc = 0.125 if seg in ("aq", "cq") else 1.0
                                P.op("act", lambda e, half=half, p_t=p_t, s_t=s_t, sc=sc: e.activation(out=s_t.h[:, half, :], in_=p_t.h[:], func=AF.Copy, scale=sc),
                                     r=[p_t], w=[s_t])
                        dst = {"aq": qaT, "ak": kaT, "cq": qcT, "ck": kcT, "mg": mgT}[seg]
                        P.dma("pool", dst[bi * 256:(bi + 1) * 256, tt * 512:(tt + 1) * 512].rearrange("(h p) t -> p h t", p=128), s_t.h[:],
                              r=[s_t], w=[R["proj"]])
                    needT = (not isF) or ((not sample) and seg in ("ak", "ck") and not DEBUG.get('noO'))
                    if needT:
                        isV = seg in ("av", "cv")
                        wantO = (not sample) and seg in ("ak", "av", "ck", "cv") and not DEBUG.get('noO')
                        if isV:
                            s_t = stV[cntV % 2]; cntV += 1
                        elif not isF:
                            s_t = stT[cntT % 2]; cntT += 1
                        if wantO:
                            o_t = stO[cntO % 2]; cntO += 1
                        for s in range(4):
                            p_t = pm[pmi % 4]; pmi += 1
                            P.op("pe", [lambda e, kc=kc, s=s, p_t=p_t, w_t=w_t: e.matmul(p_t.h[:, 0:256], lhsT=hTt.h[:, kc, s * 128:(s + 1) * 128],
                                                                                        rhs=w_t.h[:, kc, wo_:wo_ + 256], start=(kc == 0), stop=(kc == 15))
                                        for kc in range(16)], r=[w_t, hTt], w=[p_t])
                            if wantO:
                                P.op("act", lambda e, s=s, p_t=p_t, o_t=o_t: e.activation(out=o_t.h[:, s, :], in_=p_t.h[:, 0:256], func=AF.Copy), r=[p_t], w=[o_t])
                            if isV:
                                P.op("act", lambda e, s=s, p_t=p_t, s_t=s_t: e.activation(out=s_t.h[:, s, :, 0:64],
                                                                                          in_=p_t.h[:, 0:256].rearrange("p (h d) -> p h d", h=4),
                                                                                          func=AF.Copy), r=[p_t], w=[s_t])
                            elif not isF:
                                fn = AF.Silu if seg in ("ag", "bg", "cg") else AF.Copy
                                P.op("act", lambda e, s=s, p_t=p_t, s_t=s_t, fn=fn: e.activation(out=s_t.h[:, s, :], in_=p_t.h[:, 0:256], func=fn),
                                     r=[p_t], w=[s_t])
                        trange = slice(tt * 512, (tt + 1) * 512)
                        if isV:
                            dst = va if seg == "av" else vc
                            P.dma("pool", dst[trange, bi * 260:(bi + 1) * 260].rearrange("(s p) c -> p s c", p=128),
                                  s_t.h[:].rearrange("p s h d -> p s (h d)"), r=[s_t], w=[R["proj"]])
                        elif not isF:
                            dst = {"ag": ag, "bu": bu, "bg": bg, "cg": cg}[seg]
                            P.dma("pool", dst[trange, bi * 256:(bi + 1) * 256].rearrange("(s p) c -> p s c", p=128), s_t.h[:],
                                  r=[s_t], w=[R["proj"]])
                        if wantO:
                            dsto = {"ak": nak, "av": nav, "ck": nck, "cv": ncv}[seg]
                            for s in range(4):
                                sq = (tt - 8) * 2 + s // 2
                                p0 = (s % 2) * 128
                                r0 = (sq * 2 + l) * 256 + p0
                                P.dma("sp", dsto[r0:r0 + 128, bi * 256:(bi + 1) * 256], o_t.h[:, s, :], r=[o_t])
        finish()

    def phase_attn(l, which, sample, conv_args=None):
        isA = (which == "A")
        nkv = 4 if isA else 12
        gq = 3 if isA else 1
        qT_d, kT_d, v_d, g_d, y_d = (qaT, kaT, va, ag, yaT) if isA else (qcT, kcT, vc, cg, ycT)
        vw = nkv * 65
        maxch = (3 if isA else 5) if sample else 2
        with ExitStack() as st:
            idb = tile(st, [128, 128], BF16)
            P.dma("sp", idb.h[:], ident_d, w=[idb])
            Qt = [tile(st, [64, 12, 128], BF16) for _ in range(2)]
            Kt = [tile(st, [64, nkv, maxch * 128], BF16) for _ in range(2)]
            Vt = [tile(st, [128, maxch, vw], BF16) for _ in range(2)]
            Gt = [tile(st, [128, 768], BF16) for _ in range(2)]
            pt = [tile(st, [128, (3 if isA else 4) * 128], BF16) for _ in range(4)]
            dn = tile(st, [128, 12], F32)
            yn = tile(st, [128, 12, 64], F32)
            yg = tile(st, [128, 768], BF16)
            ys = [tile(st, [128, 6, 128], BF16) for _ in range(2)]
            pS = [ptile(st, [128, 512]) for _ in range(2)]
            pO = [[ptile(st, [128, 6, 65]) for _ in range(2)] for _ in range(2)]
            pY = ptile(st, [128, 6, 128], BF16)
            if sample:
                ck_d, cv_d = (cak, cav) if isA else (cck, ccv)
                kf = tile(st, [128, 4, nkv * 64], F32)
                kb = tile(st, [128, 4, nkv * 64], BF16)
                KTc = tile(st, [64, nkv, 512], BF16)
                Vc = tile(st, [128, 4, nkv, 65], BF16)
                P.dma("sp", kf.h[:], ck_d[l].rearrange("(c p) x -> p c x", p=128), w=[kf])
                P.op("dve", lambda e: e.tensor_copy(out=kb.h[:], in_=kf.h[:]), r=[kf], w=[kb])
                for h in range(nkv):
                    P.op("pe", [lambda e, c=c, h=h: e.transpose(out=pY.h[0:64, c, :], in_=kb.h[:, c, h * 64:(h + 1) * 64], identity=idb.h[:])
                                for c in range(4)], r=[kb, idb], w=[pY])
                    P.op("act", lambda e, h=h: e.activation(out=KTc.h[:, h, :].rearrange("p (c k) -> p c k", c=4), in_=pY.h[0:64, 0:4, :], func=AF.Copy),
                         r=[pY], w=[KTc])
                P.dma("sp", kf.h[:], cv_d[l].rearrange("(c p) x -> p c x", p=128), w=[kf])
                P.op("pool", lambda e: e.memset(Vc.h[:], 1.0), w=[Vc])
                for c in range(4):
                    P.op("dve", lambda e, c=c: e.tensor_copy(out=Vc.h[:, c, :, 0:64], in_=kf.h[:, c, :].rearrange("p (h d) -> p h d", h=nkv)),
                         r=[kf], w=[Vc])
                if isA:
                    trif = tile(st, [128, 2, 384], F32)
                    trib = tile(st, [128, 2, 384], BF16)
                    P.dma("sp", trif.h[:], tri_d, w=[trif])
                    P.op("dve", lambda e: e.tensor_copy(out=trib.h[:], in_=trif.h[:]), r=[trif], w=[trib])
                else:
                    nbi = tile(st, [128, 12, 5, 128], BF16)
                    nbe = tile(st, [128, 12, 4, 128], BF16)
                    for h0 in range(0, 12, 3):
                        P.dma("pool", nbi.h[:, h0:h0 + 3].rearrange("p h c q -> p (h c q)"), nab_i[l, :, h0 * 640:(h0 + 3) * 640], w=[nbi],
                              max_dma_last_dim=4096)
            if isA:
                esk = tile(st, [128, 12], F32)
                P.dma("sp", esk.h[:], a_sink[l].partition_broadcast(128), w=[esk])
                P.op("act", lambda e: e.activation(out=esk.h[:], in_=esk.h[:], func=AF.Exp), r=[esk], w=[esk])
            if sample:
                qtiles = [(i * 128,) for i in range(32)]
            else:
                qtiles = [(NS + 256 * s + 128 * a,) for s in range(4) for a in range(2)]
            hg = 3 if isA else 4
            ngrp = 12 // hg
            ctr = {"n": 0}
            conv = make_conv(l, conv_args[0], conv_args[1], st) if conv_args is not None else None

            def finalize(qi, q0, O, G):
                for b in range(2):
                    if isA:
                        P.op("dve", lambda e, b=b, O=O: e.tensor_tensor(out=dn.h[:, b * 6:(b + 1) * 6], in0=O[b].h[:, :, 64], in1=esk.h[:, b * 6:(b + 1) * 6],
                                                                        op=ALU.add), r=[O[b], esk], w=[dn])
                    else:
                        P.op("act", lambda e, b=b, O=O: e.activation(out=dn.h[:, b * 6:(b + 1) * 6], in_=O[b].h[:, :, 64], func=AF.Copy), r=[O[b]], w=[dn])
                P.op("dve", lambda e: e.reciprocal(out=dn.h[:], in_=dn.h[:]), r=[dn], w=[dn])
                for b in range(2):
                    P.op("dve", lambda e, b=b, O=O: e.tensor_tensor(out=yn.h[:, b * 6:(b + 1) * 6, :], in0=O[b].h[:, :, 0:64],
                                                                    in1=dn.h[:, b * 6:(b + 1) * 6].unsqueeze(2).to_broadcast([128, 6, 64]), op=ALU.mult),
                         r=[O[b], dn], w=[yn])
                P.op("dve", lambda e, G=G: e.tensor_tensor(out=yg.h[:], in0=yn.h[:].rearrange("p h d -> p (h d)"), in1=G.h[:], op=ALU.mult),
                     r=[yn, G], w=[yg])
                P.op("pe", [lambda e, k=k: e.transpose(out=pY.h[:, k, :], in_=yg.h[:, k * 128:(k + 1) * 128], identity=idb.h[:]) for k in range(6)],
                     r=[yg, idb], w=[pY])
                y_s = ys[qi % 2]
                P.op("act", lambda e, y_s=y_s: e.activation(out=y_s.h[:], in_=pY.h[:], func=AF.Copy), r=[pY], w=[y_s])
                P.dma("pool", y_d[:, q0:q0 + 128].rearrange("(c p) t -> p c t", p=128), y_s.h[:], r=[y_s], w=[R["mix"]])

            def steps():
                for qi, (q0,) in enumerate(qtiles):
                    interior = False
                    if sample and isA:
                        i = qi
                        lo, hi = max(0, i - 1), min(31, i + 1)
                        chunks = [("loc", c - lo, (0 if c == i - 1 else (1 if c == i + 1 else None))) for c in range(lo, hi + 1)]
                    elif sample:
                        j = qi
                        rs0 = min(max(2 * j - 4, 0), 56); rs1 = min(max(2 * j + 1 - 4, 0), 56)
                        lo, hi = rs0 // 2, (rs1 + 7) // 2
                        interior = 2 <= j <= 29
                        et = {0: 0, 1: 1, 30: 2, 31: 3}.get(j)
                        chunks = [("loc", c - lo, ((c - j + 2) if interior else (c - lo))) for c in range(lo, hi + 1)]
                        if not interior:
                            for h0 in range(0, 12, 3):
                                P.dma("pool", nbe.h[:, h0:h0 + 3].rearrange("p h c q -> p (h c q)"), nab_e[l, et, :, h0 * 512:(h0 + 3) * 512], w=[nbe],
                                      max_dma_last_dim=4096)
                    else:
                        sq0 = q0 - ((q0 - NS) % 256)
                        lo, hi = sq0 // 128, sq0 // 128 + 1
                        chunks = [("loc", 0, None), ("loc", 1, None)]
                    if sample:
                        chunks = chunks + [("ctx", c, None) for c in range(4)]
                    nloc = hi - lo + 1
                    Q = Qt[qi % 2]; K = Kt[qi % 2]; V = Vt[qi % 2]; G = Gt[qi % 2]
                    P.dma("sp", Q.h[:], qT_d[:, q0:q0 + 128].rearrange("(h d) t -> d h t", d=64), r=[R["proj"]], w=[Q])
                    P.dma("sp", K.h[:, :, 0:nloc * 128], kT_d[:, lo * 128:(hi + 1) * 128].rearrange("(h d) t -> d h t", d=64), r=[R["proj"]], w=[K])
                    P.dma("sp", V.h[:, 0:nloc, :], v_d[lo * 128:(hi + 1) * 128, :].rearrange("(c p) x -> p c x", p=128), r=[R["proj"]], w=[V])
                    P.dma("sp", G.h[:], g_d[q0:q0 + 128, :], r=[R["proj"]], w=[G])
                    O = pO[qi % 2]
                    started = [False, False]
                    nst = ngrp * len(chunks)
                    k = 0
                    for g in range(ngrp):
                        heads = list(range(g * hg, (g + 1) * hg))
                        for (kind, cx, mask) in chunks:
                            k += 1
                            n = ctr["n"]; ctr["n"] += 1
                            p_s = pS[n % 2]; p_t = pt[n % 4]
                            Ksrc = K if kind == "loc" else KTc
                            Vsrc = V if kind == "loc" else Vc
                            sf = []
                            if isA:
                                sf.append(lambda e, p_s=p_s, Ksrc=Ksrc, g=g, cx=cx, Q=Q, mask=mask: e.matmul(
                                    p_s.h[:, 0:384], lhsT=Ksrc.h[:, g, cx * 128:(cx + 1) * 128], rhs=Q.h[:, 3 * g:3 * g + 3, :], start=True, stop=(mask is None)))
                            else:
                                for jh, hd in enumerate(heads):
                                    sf.append(lambda e, p_s=p_s, Ksrc=Ksrc, hd=hd, jh=jh, cx=cx, Q=Q, mask=mask: e.matmul(
                                        p_s.h[:, jh * 128:(jh + 1) * 128], lhsT=Ksrc.h[:, hd, cx * 128:(cx + 1) * 128], rhs=Q.h[:, hd, :],
                                        start=(jh == 0), stop=(mask is None and jh == hg - 1), skip_group_check=True))
                            rr = [Ksrc, Q]
                            if mask is not None:
                                if isA:
                                    m_ap = trib.h[:, mask, :]; rr.append(trib)
                                elif interior:
                                    m_ap = nbi.h[:, g * hg:(g + 1) * hg, mask, :]; rr.append(nbi)
                                else:
                                    m_ap = nbe.h[:, g * hg:(g + 1) * hg, mask, :]; rr.append(nbe)
                                rr.append(idb)
                                sf.append(lambda e, p_s=p_s, m_ap=m_ap: e.matmul(p_s.h[:, 0:hg * 128], lhsT=idb.h[:], rhs=m_ap, start=False, stop=True,
                                                                               skip_group_check=True))
                            pf = []
                            banks = []
                            for jh, hd in enumerate(heads):
                                ob = O[hd // 6]
                                if ob not in banks:
                                    banks.append(ob)
                                kvh = g if isA else hd
                                v_ap = Vsrc.h[:, cx, kvh * 65:(kvh + 1) * 65] if kind == "loc" else Vsrc.h[:, cx, kvh, :]
                                first = not started[hd // 6]
                                started[hd // 6] = True
                                pf.append(lambda e, ob=ob, hd=hd, jh=jh, p_t=p_t, v_ap=v_ap, first=first: e.matmul(
                                    ob.h[:, hd % 6, :], lhsT=p_t.h[:, jh * 128:(jh + 1) * 128], rhs=v_ap, start=first, stop=False, skip_group_check=True))
                            yield dict(sf=sf, rr=rr, p_s=p_s, p_t=p_t, pf=pf, pr=[p_t, Vsrc], banks=banks, last=(k == nst), fin=(qi, q0, O, G))

            it = steps()
            cur = next(it)
            P.op("pe", cur["sf"], r=cur["rr"], w=[cur["p_s"]])
            while cur is not None:
                nxt = next(it, None)
                if nxt is not None:
                    P.op("pe", nxt["sf"], r=nxt["rr"], w=[nxt["p_s"]])
                P.op("act", lambda e, p_s=cur["p_s"], p_t=cur["p_t"]: e.activation(out=p_t.h[:], in_=p_s.h[:, 0:hg * 128], func=AF.Exp),
                     r=[cur["p_s"]], w=[cur["p_t"]])
                P.op("pe", cur["pf"], r=cur["pr"], w=cur["banks"])
                if cur["last"]:
                    finalize(*cur["fin"])
                    if conv is not None:
                        conv(cur["fin"][0])
                cur = nxt
        finish()

    def make_conv(l, tag, VZ, st1):
        tb = tabs[tag]
        L, n = tb["L"], tb["n"]
        base = 0 if tag == "s" else NS
        cwb = tile(st1, [128, 4, 1536], F32)
        for j in range(3):
            P.dma("sp", cwb.h[:, j, :], hy_conv_w[l, j].partition_broadcast(128), w=[cwb])
        P.dma("sp", cwb.h[:, 3, :], hy_conv_b[l].partition_broadcast(128), w=[cwb])
        sh3 = [[tile(st1, [128, 1536], BF16) for _ in range(3)] for _ in range(2)]
        ca = tile(st1, [128, 1536], F32); cb_ = tile(st1, [128, 1536], F32)
        x12 = [tile(st1, [128, 2, 512], BF16) for _ in range(2)]

        def conv(k):
            sq, i = k // n, k % n
            t0 = base + sq * L + i * 128
            a, b, c = sh3[k % 2]
            if i == 0:
                P.op("pool", lambda e, a=a: e.memset(a.h[:], 0.0), w=[a])
                P.dma("sp", a.h[1:128, :], bu[t0:t0 + 127, :], r=[R["proj"]], w=[a])
            else:
                P.dma("sp", a.h[:], bu[t0 - 1:t0 + 127, :], r=[R["proj"]], w=[a])
            P.dma("sp", b.h[:], bu[t0:t0 + 128, :], r=[R["proj"]], w=[b])
            if i == n - 1:
                P.op("pool", lambda e, c=c: e.memset(c.h[:], 0.0), w=[c])
                P.dma("sp", c.h[0:127, :], bu[t0 + 1:t0 + 128, :], r=[R["proj"]], w=[c])
            else:
                P.dma("sp", c.h[:], bu[t0 + 1:t0 + 129, :], r=[R["proj"]], w=[c])
            P.op("dve", lambda e, a=a: e.tensor_tensor(out=ca.h[:], in0=a.h[:], in1=cwb.h[:, 0, :], op=ALU.mult), r=[a, cwb], w=[ca])
            P.op("dve", lambda e: e.tensor_tensor(out=ca.h[:], in0=ca.h[:], in1=cwb.h[:, 3, :], op=ALU.add), r=[ca, cwb], w=[ca])
            P.op("dve", lambda e, b=b: e.tensor_tensor(out=cb_.h[:], in0=b.h[:], in1=cwb.h[:, 1, :], op=ALU.mult), r=[b, cwb], w=[cb_])
            P.op("dve", lambda e: e.tensor_tensor(out=ca.h[:], in0=ca.h[:], in1=cb_.h[:], op=ALU.add), r=[ca, cb_], w=[ca])
            P.op("dve", lambda e, c=c: e.tensor_tensor(out=cb_.h[:], in0=c.h[:], in1=cwb.h[:, 2, :], op=ALU.mult), r=[c, cwb], w=[cb_])
            x_t = x12[k % 2]
            P.op("dve", lambda e, i=i, sq=sq: e.tensor_tensor(out=VZ.h[:, i, sq * 512:(sq + 1) * 512], in0=ca.h[:, 0:512], in1=cb_.h[:, 0:512], op=ALU.add),
                 r=[ca, cb_], w=[VZ])
            P.op("dve", lambda e, x_t=x_t: e.tensor_tensor(out=x_t.h[:].rearrange("p a c -> p (a c)"), in0=ca.h[:, 512:1536], in1=cb_.h[:, 512:1536], op=ALU.add),
                 r=[ca, cb_], w=[x_t])
            P.dma("pool", x1c[t0:t0 + 128, :], x_t.h[:, 0, :], r=[x_t], w=[R["x12"]])
            P.dma("pool", x2c[t0:t0 + 128, :], x_t.h[:, 1, :], r=[x_t], w=[R["x12"]])
        return conv

    def phase_hyena(l, tag, VZ):
        tb = tabs[tag]
        L, n = tb["L"], tb["n"]
        nseq = 1 if tag == "s" else 4
        base = 0 if tag == "s" else NS
        with ExitStack() as st:
            idb = tile(st, [128, 128], BF16)
            P.dma("sp", idb.h[:], ident_d, w=[idb])
            altc = tile(st, [128, n], BF16); altr = tile(st, [1, L], BF16)
            P.dma("sp", altc.h[:], tb["altc"], w=[altc]); P.dma("sp", altr.h[:], tb["altr"], w=[altr])
            skb = tile(st, [128, 2, 512], F32)
            for o in range(2):
                P.dma("sp", skb.h[:, o, :], hy_skip[l, o].partition_broadcast(128), w=[skb])
            spn = tile(st, [1, 2, 512], F32)
            P.dma("sp", spn.h[:], specn[tag][l].rearrange("(x o) c -> x o c", x=1), r=[R["spec"]], w=[spn])
            pr = [ptile(st, [128, 512]) for _ in range(2)]
            pi = [ptile(st, [128, 512]) for _ in range(2)]
            py = [ptile(st, [128, 512]) for _ in range(2)]
            pY = ptile(st, [128, 4, 128], BF16)
            Ya = tile(st, [128, n, nseq * 512], BF16); Yb = tile(st, [128, n, nseq * 512], BF16)
            Yn = tile(st, [1, nseq * 512], BF16)
            ct = [tile(st, [128, n, 128], BF16) for _ in range(2)]
            stt = [tile(st, [128, n, 128], BF16) for _ in range(2)]
            sr = [tile(st, [128, 512], F32) for _ in range(2)]; si = [tile(st, [128, 512], F32) for _ in range(2)]
            t1 = tile(st, [128, 512], F32); t2 = tile(st, [128, 512], F32)
            xin_t = [tile(st, [128, 512], BF16) for _ in range(2)]
            bgt = [tile(st, [128, 512], BF16) for _ in range(2)]
            yb_t = tile(st, [128, 512], BF16)
            ybs = [tile(st, [128, 4, 128], BF16) for _ in range(2)]
            for o in range(2):
                for j in range(n):
                    c_t, s_t = ct[j % 2], stt[j % 2]
                    load_tab(tb, j, c_t, s_t)
                    s_r, s_i = sr[j % 2], si[j % 2]
                    P.dma("sp", s_r.h[:], spec[tag][l, o, 0, j * 128:(j + 1) * 128, :], r=[R["spec"]], w=[s_r])
                    P.dma("sp", s_i.h[:], spec[tag][l, o, 1, j * 128:(j + 1) * 128, :], r=[R["spec"]], w=[s_i])
                    for sq in range(nseq):
                        cs = slice(sq * 512, (sq + 1) * 512)
                        p_r, p_i = pr[(j * nseq + sq) % 2], pi[(j * nseq + sq) % 2]
                        P.op("pe", [lambda e, cc=cc, p_r=p_r, c_t=c_t, cs=cs: e.matmul(p_r.h[:], lhsT=c_t.h[:, cc, :], rhs=VZ.h[:, cc, cs],
                                                                                    start=(cc == 0), stop=(cc == n - 1)) for cc in range(n)],
                             r=[c_t, VZ], w=[p_r])
                        P.op("pe", [lambda e, cc=cc, p_i=p_i, s_t=s_t, cs=cs: e.matmul(p_i.h[:], lhsT=s_t.h[:, cc, :], rhs=VZ.h[:, cc, cs],
                                                                                    start=(cc == 0), stop=(cc == n - 1)) for cc in range(n)],
                             r=[s_t, VZ], w=[p_i])
                        P.op("dve", lambda e, p_r=p_r, s_r=s_r: e.tensor_tensor(out=t1.h[:], in0=p_r.h[:], in1=s_r.h[:], op=ALU.mult), r=[p_r, s_r], w=[t1])
                        P.op("dve", lambda e, p_i=p_i, s_i=s_i: e.tensor_tensor(out=t2.h[:], in0=p_i.h[:], in1=s_i.h[:], op=ALU.mult), r=[p_i, s_i], w=[t2])
                        P.op("dve", lambda e, j=j, cs=cs: e.tensor_tensor(out=Ya.h[:, j, cs], in0=t1.h[:], in1=t2.h[:], op=ALU.add), r=[t1, t2], w=[Ya])
                        P.op("dve", lambda e, p_i=p_i, s_r=s_r: e.tensor_tensor(out=t1.h[:], in0=p_i.h[:], in1=s_r.h[:], op=ALU.mult), r=[p_i, s_r], w=[t1])
                        P.op("dve", lambda e, p_r=p_r, s_i=s_i: e.tensor_tensor(out=t2.h[:], in0=p_r.h[:], in1=s_i.h[:], op=ALU.mult), r=[p_r, s_i], w=[t2])
                        P.op("dve", lambda e, j=j, cs=cs: e.tensor_tensor(out=Yb.h[:, j, cs], in0=t1.h[:], in1=t2.h[:], op=ALU.subtract), r=[t1, t2], w=[Yb])
                for sq in range(nseq):
                    cs = slice(sq * 512, (sq + 1) * 512)
                    p_r = pr[sq % 2]
                    P.op("pe", [lambda e, cc=cc, p_r=p_r, cs=cs: e.matmul(p_r.h[0:1, :], lhsT=altc.h[:, cc:cc + 1], rhs=VZ.h[:, cc, cs],
                                                                        start=(cc == 0), stop=(cc == n - 1)) for cc in range(n)], r=[altc, VZ], w=[p_r])
                    P.op("dve", lambda e, p_r=p_r, cs=cs, o=o: e.tensor_tensor(out=Yn.h[:, cs], in0=p_r.h[0:1, :], in1=spn.h[:, o, :], op=ALU.mult),
                         r=[p_r, spn], w=[Yn])
                for i in range(n):
                    c_t, s_t = ct[(n + i) % 2], stt[(n + i) % 2]
                    load_tab(tb, i, c_t, s_t)
                    for sq in range(nseq):
                        cs = slice(sq * 512, (sq + 1) * 512)
                        t0 = base + sq * L + i * 128
                        p_y = py[(i * nseq + sq) % 2]
                        fns = [lambda e, cc=cc, p_y=p_y, c_t=c_t, cs=cs: e.matmul(p_y.h[:], lhsT=c_t.h[:, cc, :], rhs=Ya.h[:, cc, cs], start=(cc == 0), stop=False)
                               for cc in range(n)]
                        fns += [lambda e, cc=cc, p_y=p_y, s_t=s_t, cs=cs: e.matmul(p_y.h[:], lhsT=s_t.h[:, cc, :], rhs=Yb.h[:, cc, cs], start=False, stop=False)
                                for cc in range(n)]
                        fns += [lambda e, p_y=p_y, i=i, cs=cs: e.matmul(p_y.h[:], lhsT=altr.h[0:1, i * 128:(i + 1) * 128], rhs=Yn.h[0:1, cs], start=False, stop=True)]
                        P.op("pe", fns, r=[c_t, s_t, Ya, Yb, Yn, altr], w=[p_y])
                        x_t = xin_t[(i * nseq + sq) % 2]
                        P.dma("sp", x_t.h[:], (x1c if o == 0 else x2c)[t0:t0 + 128, :], r=[R["x12"]], w=[x_t])
                        P.op("dve", lambda e, i=i, cs=cs, o=o: e.tensor_tensor(out=t1.h[:], in0=VZ.h[:, i, cs], in1=skb.h[:, o, :], op=ALU.mult), r=[VZ, skb], w=[t1])
                        P.op("dve", lambda e, p_y=p_y: e.tensor_tensor(out=t1.h[:], in0=p_y.h[:], in1=t1.h[:], op=ALU.add), r=[p_y, t1], w=[t1])
                        if o == 0:
                            P.op("dve", lambda e, i=i, cs=cs, x_t=x_t: e.tensor_tensor(out=VZ.h[:, i, cs], in0=t1.h[:], in1=x_t.h[:], op=ALU.mult),
                                 r=[t1, x_t], w=[VZ])
                        else:
                            b_t = bgt[(i * nseq + sq) % 2]
                            P.dma("sp", b_t.h[:], bg[t0:t0 + 128, :], r=[R["proj"]], w=[b_t])
                            P.op("dve", lambda e, x_t=x_t: e.tensor_tensor(out=t2.h[:], in0=t1.h[:], in1=x_t.h[:], op=ALU.mult), r=[t1, x_t], w=[t2])
                            P.op("dve", lambda e, b_t=b_t: e.tensor_tensor(out=yb_t.h[:], in0=t2.h[:], in1=b_t.h[:], op=ALU.mult), r=[t2, b_t], w=[yb_t])
                            P.op("pe", [lambda e, k=k: e.transpose(out=pY.h[:, k, :], in_=yb_t.h[:, k * 128:(k + 1) * 128], identity=idb.h[:]) for k in range(4)],
                                 r=[yb_t, idb], w=[pY])
                            y_s = ybs[(i * nseq + sq) % 2]
                            P.op("act", lambda e, y_s=y_s: e.activation(out=y_s.h[:], in_=pY.h[:], func=AF.Copy), r=[pY], w=[y_s])
                            P.dma("pool", ybT[:, t0:t0 + 128].rearrange("(c p) t -> p c t", p=128), y_s.h[:], r=[y_s], w=[R["mix"]])
        finish()

    def phase_merge(l):
        last = (l == 1)
        xsrc = xin if l == 0 else xres
        with ExitStack() as st:
            wa = tile(st, [128, 6, D], BF16); wbt = tile(st, [128, 4, D], BF16); wc = tile(st, [128, 6, D], BF16)
            P.dma("sp", wa.h[:], w_upa_b[l].rearrange("(kc p) n -> p kc n", p=128), r=[R["wb"]], w=[wa])
            P.dma("sp", wbt.h[:], w_upb_b[l].rearrange("(kc p) n -> p kc n", p=128), r=[R["wb"]], w=[wbt])
            P.dma("sp", wc.h[:], w_upc_b[l].rearrange("(kc p) n -> p kc n", p=128), r=[R["wb"]], w=[wc])
            gate = tile(st, [128, D], F32)
            if last:
                fnw = tile(st, [128, D], F32)
                P.dma("sp", fnw.h[:], final_norm_w.partition_broadcast(128), w=[fnw])
            ya_t = tile(st, [128, 6, 512], BF16); yb_t = tile(st, [128, 4, 512], BF16); yc_t = tile(st, [128, 6, 512], BF16)
            mg_t = [tile(st, [128, 3, 512], BF16) for _ in range(2)]
            mT = tile(st, [128, 16, 512], BF16)
            wo = [tile(st, [128, 16, 256], BF16) for _ in range(2)]
            xt = [tile(st, [128, D], F32) for _ in range(4)]
            t1 = tile(st, [128, 512], F32); t2 = tile(st, [128, 512], F32)
            tq_ = [tile(st, [128, 256], F32) for _ in range(2)]
            junk = tile(st, [128, D], BF16)
            ss = tile(st, [128, 1], F32)
            pa2 = [ptile(st, [128, 512]) for _ in range(2)]; pb2 = [ptile(st, [128, 512]) for _ in range(2)]
            pc2 = [ptile(st, [128, 512]) for _ in range(2)]
            po = pa2
            tA = [tile(st, [128, 512], F32) for _ in range(2)]; tB = [tile(st, [128, 512], F32) for _ in range(2)]
            nwo = 0
            for tt in range(NT // 512):
                g = 0 if tt < 8 else 1
                if tt in (0, 8):
                    P.dma("sp", gate.h[:], modv[l, 1 - g, 2, :].partition_broadcast(128), r=[R["modv"]], w=[gate])
                ts_ = slice(tt * 512, (tt + 1) * 512)
                P.dma("sp", ya_t.h[:], yaT[:, ts_].rearrange("(c p) t -> p c t", p=128), r=[R["mix"]], w=[ya_t])
                P.dma("sp", yb_t.h[:], ybT[:, ts_].rearrange("(c p) t -> p c t", p=128), r=[R["mix"]], w=[yb_t])
                P.dma("sp", yc_t.h[:], ycT[:, ts_].rearrange("(c p) t -> p c t", p=128), r=[R["mix"]], w=[yc_t])
                for fo in range(16):
                    m_t = mg_t[fo % 2]
                    for b in range(3):
                        P.dma("sp", m_t.h[:, b, :], mgT[b * 2048 + fo * 128:b * 2048 + (fo + 1) * 128, ts_], r=[R["proj"]], w=[m_t])
                    fs = slice(fo * 128, (fo + 1) * 128)
                    pa, pb, pc = pa2[fo % 2], pb2[fo % 2], pc2[fo % 2]
                    t1, t2 = tA[fo % 2], tB[fo % 2]
                    P.op("pe", [lambda e, kc=kc, fs=fs: e.matmul(pa.h[:], lhsT=wa.h[:, kc, fs], rhs=ya_t.h[:, kc, :], start=(kc == 0), stop=(kc == 5)) for kc in range(6)],
                         r=[wa, ya_t], w=[pa])
                    P.op("pe", [lambda e, kc=kc, fs=fs: e.matmul(pb.h[:], lhsT=wbt.h[:, kc, fs], rhs=yb_t.h[:, kc, :], start=(kc == 0), stop=(kc == 3)) for kc in range(4)],
                         r=[wbt, yb_t], w=[pb])
                    P.op("pe", [lambda e, kc=kc, fs=fs: e.matmul(pc.h[:], lhsT=wc.h[:, kc, fs], rhs=yc_t.h[:, kc, :], start=(kc == 0), stop=(kc == 5)) for kc in range(6)],
                         r=[wc, yc_t], w=[pc])
                    P.op("dve", lambda e, m_t=m_t: e.tensor_tensor(out=t1.h[:], in0=pa.h[:], in1=m_t.h[:, 0, :], op=ALU.mult), r=[pa, m_t], w=[t1])
                    P.op("dve", lambda e, m_t=m_t: e.tensor_tensor(out=t2.h[:], in0=pb.h[:], in1=m_t.h[:, 1, :], op=ALU.mult), r=[pb, m_t], w=[t2])
                    P.op("dve", lambda e: e.tensor_tensor(out=t1.h[:], in0=t1.h[:], in1=t2.h[:], op=ALU.add), r=[t1, t2], w=[t1])
                    P.op("dve", lambda e, m_t=m_t: e.tensor_tensor(out=t2.h[:], in0=pc.h[:], in1=m_t.h[:, 2, :], op=ALU.mult), r=[pc, m_t], w=[t2])
                    P.op("dve", lambda e, fo=fo: e.tensor_tensor(out=mT.h[:, fo, :], in0=t1.h[:], in1=t2.h[:], op=ALU.add), r=[t1, t2], w=[mT])
                for s in range(4):
                    tok0 = tt * 512 + s * 128
                    P.dma("sp", xt[s].h[:], xsrc[tok0:tok0 + 128, :], r=[R["xres"]], w=[xt[s]])
                for cbk in range(8):
                    w_t = wo[nwo % 2]; nwo += 1
                    cs = slice(cbk * 256, (cbk + 1) * 256)
                    P.dma("sp", w_t.h[:], w_out_b[l, :, cs].rearrange("(kc p) n -> p kc n", p=128), r=[R["wb"]], w=[w_t])
                    for s in range(4):
                        p_o = po[s % 2]
                        P.op("pe", [lambda e, kc=kc, s=s, p_o=p_o, w_t=w_t: e.matmul(p_o.h[:, 0:256], lhsT=mT.h[:, kc, s * 128:(s + 1) * 128], rhs=w_t.h[:, kc, :],
                                                                                    start=(kc == 0), stop=(kc == 15)) for kc in range(16)],
                             r=[mT, w_t], w=[p_o])
                        tq = tq_[s % 2]
                        P.op("dve", lambda e, p_o=p_o, cs=cs, tq=tq: e.tensor_tensor(out=tq.h[:, 0:256], in0=p_o.h[:, 0:256], in1=gate.h[:, cs], op=ALU.mult),
                             r=[p_o, gate], w=[tq])
                        P.op("dve", lambda e, s=s, cs=cs, tq=tq: e.tensor_tensor(out=xt[s].h[:, cs], in0=xt[s].h[:, cs], in1=tq.h[:, 0:256], op=ALU.add),
                             r=[xt[s], tq], w=[xt[s]])
                for s in range(4):
                    tok0 = tt * 512 + s * 128
                    x_t = xt[s]
                    if not last:
                        P.dma("pool", xres[tok0:tok0 + 128, :], x_t.h[:], r=[x_t], w=[R["xres"]])
                    else:
                        P.op("act", lambda e, x_t=x_t: e.activation(out=junk.h[:], in_=x_t.h[:], func=AF.Square, accum_out=ss.h[:]), r=[x_t], w=[junk, ss])
                        P.op("act", lambda e: e.activation(out=ss.h[:], in_=ss.h[:], func=AF.Sqrt, scale=1.0 / D, bias=EPS), r=[ss], w=[ss])
                        P.op("dve", lambda e: e.reciprocal(out=ss.h[:], in_=ss.h[:]), r=[ss], w=[ss])
                        P.op("dve", lambda e, x_t=x_t: e.scalar_tensor_tensor(out=x_t.h[:], in0=x_t.h[:], scalar=ss.h[:, 0:1], in1=fnw.h[:], op0=ALU.mult, op1=ALU.mult),
                             r=[x_t, ss, fnw], w=[x_t])
                        P.dma("pool", y_out[tok0:tok0 + 128, :], x_t.h[:], r=[x_t])
        finish()

    if "prep" in phases:
        phase_cast()
        phase_mod()
    if "filt" in phases:
        phase_filt("s")
        phase_filt("p")
    for l in layers:
        if "proj" in phases:
            phase_proj(l)
        if "attn" in phases:
            lst = ExitStack()
            VZs = tile(lst, [128, 32, 512], BF16)
            VZp = tile(lst, [128, 2, 2048], BF16)
            for (wh, smp) in DEBUG.get("attn", (("A", True), ("A", False), ("C", True), ("C", False))):
                phase_attn(l, wh, smp, conv_args=((("s", VZs) if smp else ("p", VZp)) if wh == "A" else None))
            phase_hyena(l, "s", VZs)
            phase_hyena(l, "p", VZp)
            lst.close()
        if "merge" in phases:
            phase_merge(l)
    P.close()
    return nc, P


_CONSTS = None
_NC = None


def _get_nc():
    global _NC
    if _NC is None:
        _NC = build()[0]
    return _NC


def _make_in_maps(inp):
    global _CONSTS
    if _CONSTS is None:
        _CONSTS = _consts()
    f = lambda a: np.ascontiguousarray(np.asarray(a, dtype=np.float32))
    shared = {}
    for k in ("c_ctx", "norm_w", "w_ada", "b_ada", "w_in", "a_sink", "hy_conv_w", "hy_conv_b", "hy_w1", "hy_b1", "hy_w2", "hy_b2",
              "hy_freq", "hy_w3", "hy_skip", "w_up_a", "w_up_b", "w_up_c", "w_out", "final_norm_w"):
        shared[k] = f(inp[k])
    shared["hy_decay"] = f(inp["hy_decay"]).reshape(2, 2048)
    rpb = f(inp["c_rpb"])
    ni, ne = [], []
    for l in range(2):
        a, b = _na_tables(rpb[l])
        ni.append(a.reshape(128, -1)); ne.append(b.reshape(4, 128, -1))
    shared["nab_i"] = np.ascontiguousarray(np.stack(ni)); shared["nab_e"] = np.ascontiguousarray(np.stack(ne))
    shared.update(_CONSTS)
    xs = f(inp["x_sample"]); xp = f(inp["x_prompt"]); c = f(inp["c"])
    cak = f(inp["cache_a_k"]); cav = f(inp["cache_a_v"]); cck = f(inp["cache_c_k"]); ccv = f(inp["cache_c_v"])
    maps = []
    for i in range(8):
        m = dict(shared)
        m["xin"] = np.ascontiguousarray(np.concatenate([xs[i], xp[4 * i:4 * i + 4].reshape(NP, D)], axis=0))
        m["c_lat"] = np.ascontiguousarray(c[i])
        m["cak"] = np.ascontiguousarray(cak[i].reshape(2, 512, 256)); m["cav"] = np.ascontiguousarray(cav[i].reshape(2, 512, 256))
        m["cck"] = np.ascontiguousarray(cck[i].reshape(2, 512, 768)); m["ccv"] = np.ascontiguousarray(ccv[i].reshape(2, 512, 768))
        maps.append(m)
    return maps


def kernel(**inp):
    nc = _get_nc()
    maps = _make_in_maps(inp)
    res = run_bass_kernel_spmd(nc, maps, core_ids=list(range(8)))
    rs = res.results
    y_prompt = np.concatenate([r["y"][NS:].reshape(4, 256, D) for r in rs], axis=0)
    y_sample = np.stack([r["y"][:NS] for r in rs], axis=0)
    nak = np.concatenate([r["nak"].reshape(4, 2, 256, 4, 64) for r in rs], axis=0)
    nav = np.concatenate([r["nav"].reshape(4, 2, 256, 4, 64) for r in rs], axis=0)
    nck = np.concatenate([r["nck"].reshape(4, 2, 256, 12, 64) for r in rs], axis=0)
    ncv = np.concatenate([r["ncv"].reshape(4, 2, 256, 12, 64) for r in rs], axis=0)
    return (y_prompt.astype(np.float32), y_sample.astype(np.float32), nak.astype(np.float32), nav.astype(np.float32),
            nck.astype(np.float32), ncv.astype(np.float32))
```

```python
import types
import numpy as np
import ml_dtypes
from contextlib import ExitStack
import concourse.bass as bass
import concourse.mybir as mybir
from concourse.bass_utils import run_bass_kernel_spmd

F32 = mybir.dt.float32
BF16 = mybir.dt.bfloat16
AF = mybir.ActivationFunctionType
ALU = mybir.AluOpType
ENGS = ["pe", "act", "dve", "pool", "sp"]

D = 2048
NS = 4096
NP = 1024
NT = NS + NP
INW = 13312
NEG = -30000.0
EPS = 1e-6


class Res:
    __slots__ = ("last_w", "readers")

    def __init__(self):
        self.last_w = None
        self.readers = []


def _freeze(fn):
    if getattr(fn, "__closure__", None) is None:
        return fn
    cells = []
    for c in fn.__closure__:
        try:
            cells.append(types.CellType(c.cell_contents))
        except ValueError:
            cells.append(c)
    g = types.FunctionType(fn.__code__, fn.__globals__, fn.__name__, fn.__defaults__, tuple(cells))
    g.__kwdefaults__ = fn.__kwdefaults__
    return g


class Prog:
    def __init__(self, nc):
        self.nc = nc
        self.stack = ExitStack()
        self.eng = {"pe": nc.tensor, "act": nc.scalar, "dve": nc.vector, "pool": nc.gpsimd, "sp": nc.sync}
        self.sem = {e: self.stack.enter_context(nc.semaphore("c_" + e)) for e in ENGS}
        nd = {"sp": 28, "pool": 20, "act": 4}
        self.dsem = {q: [self.stack.enter_context(nc.semaphore(f"d_{q}{i}")) for i in range(n)] for q, n in nd.items()}
        self.dcnt = {q: 0 for q in self.dsem}
        self.dval = {q: [0] * len(v) for q, v in self.dsem.items()}
        self.cnt = {e: 0 for e in ENGS}
        self.q = {e: [] for e in ENGS}
        self.waited = {}
        self.pending_dma = []
        self.ninst = 0

    def _need(self, eng, tok, waits):
        if tok is None:
            return
        if tok[0] == "e":
            _, e2, idx = tok
            if e2 == eng and eng == "pe":
                return
            key = (eng, "e", e2)
            if self.waited.get(key, 0) >= idx:
                return
            self.waited[key] = idx
            waits.append((self.sem[e2], idx))
        else:
            _, qn, si, target = tok
            key = (eng, "d", qn, si)
            if self.waited.get(key, 0) >= target:
                return
            self.waited[key] = target
            waits.append((self.dsem[qn][si], target))

    def _deps(self, eng, r, w):
        waits = []
        for res in r:
            self._need(eng, res.last_w, waits)
        for res in w:
            self._need(eng, res.last_w, waits)
            for t in res.readers:
                self._need(eng, t, waits)
        return waits

    def op(self, eng, fns, r=(), w=()):
        if not isinstance(fns, (list, tuple)):
            fns = [fns]
        r = [x.r if hasattr(x, "r") else x for x in r]
        w = [x.r if hasattr(x, "r") else x for x in w]
        waits = self._deps(eng, r, w)
        self.cnt[eng] += 1
        tok = ("e", eng, self.cnt[eng])
        self.q[eng].append((waits, [_freeze(f) for f in fns], (self.sem[eng], 1)))
        for res in r:
            res.readers.append(tok)
        for res in w:
            res.last_w = tok
            res.readers = []
        self.ninst += len(fns)
        return tok

    def dma(self, qn, out, in_, r=(), w=(), **kw):
        r = [x.r if hasattr(x, "r") else x for x in r]
        w = [x.r if hasattr(x, "r") else x for x in w]
        waits = self._deps(qn, r, w)
        n = self.dcnt[qn]
        self.dcnt[qn] += 1
        si = n % len(self.dsem[qn])
        prev = self.dval[qn][si]
        if prev > 0:
            key = (qn, "d", qn, si)
            if self.waited.get(key, 0) < prev:
                self.waited[key] = prev
                waits.append((self.dsem[qn][si], prev))
        target = prev + 16
        self.dval[qn][si] = target
        tok = ("d", qn, si, target)
        self.q[qn].append((waits, [lambda e, o=out, i=in_, k=kw: e.dma_start(out=o, in_=i, **k)], (self.dsem[qn][si], 16)))
        for res in r:
            res.readers.append(tok)
        for res in w:
            res.last_w = tok
            res.readers = []
        self.pending_dma.append(tok)
        self.ninst += 1
        return tok

    def barrier(self):
        toks = [("e", e, self.cnt[e]) for e in ENGS if self.cnt[e] > 0]
        best = {}
        for t in self.pending_dma:
            k = (t[1], t[2])
            if k not in best or best[k][3] < t[3]:
                best[k] = t
        toks += list(best.values())
        for eng in ENGS:
            waits = []
            for t in toks:
                if t[0] == "e" and t[1] == eng:
                    continue
                self._need(eng, t, waits)
            if waits:
                self.q[eng].append((waits, [], None))
        self.pending_dma = []

    def emit(self):
        nc = self.nc
        qs = self.q
        self.q = {e: [] for e in ENGS}

        def run(engname):
            def f(e):
                for waits, fns, inc in qs[engname]:
                    for s, v in waits:
                        e.wait_ge(s, v)
                    last = None
                    for fn in fns:
                        last = fn(e)
                    if inc is not None and last is not None:
                        last.then_inc(inc[0], inc[1])
            return f

        with nc.Block() as block:
            block.tensor(run("pe"))
            block.scalar(run("act"))
            block.vector(run("dve"))
            block.gpsimd(run("pool"))
            block.sync(run("sp"))

    def close(self):
        self.stack.close()


class Tl:
    __slots__ = ("h", "r")

    def __init__(self, h):
        self.h = h
        self.r = Res()


_uid = [0]


def _name(p):
    _uid[0] += 1
    return f"{p}{_uid[0]}"


def _bf(a):
    return np.ascontiguousarray(a.astype(ml_dtypes.bfloat16))


def _dft_tables(L):
    N = 2 * L
    n = L // 128
    t = np.arange(L, dtype=np.int64)
    m = (t[:, None] * t[None, :]) % N
    ang = (2.0 * np.pi / N) * m.astype(np.float64)
    C = np.cos(ang)
    S = np.sin(ang)

    def tile(M):
        return _bf(M.reshape(n, 128, n, 128).transpose(2, 1, 0, 3))
    return tile(C), tile(S)


def _consts():
    cs = {}
    cs["ident"] = _bf(np.eye(128))
    cs["identf"] = np.eye(128, dtype=np.float32)
    pm = np.zeros((128, 128))
    for m in range(128):
        k = m + 32 if (m % 64) < 32 else m - 32
        pm[k, m] = 1.0
    cs["perm"] = _bf(pm)
    t = np.arange(NS)
    row = (t // 64).astype(np.float64)
    col = (t % 64).astype(np.float64)
    nf = 16
    inv = np.power(10000.0, -np.arange(nf, dtype=np.float64) / nf)
    ang = np.concatenate([row[None, :] * inv[:, None], col[None, :] * inv[:, None]], axis=0)
    cosT = np.cos(ang)
    sinT = np.sin(ang)
    rc = np.zeros((128, NS))
    rs = np.zeros((128, NS))
    for p in range(128):
        rc[p] = cosT[p % 32]
        rs[p] = -sinT[p % 32] if (p % 64) < 32 else sinT[p % 32]
    cs["ropeC"] = rc.astype(np.float32)
    cs["ropeS"] = rs.astype(np.float32)
    k = np.arange(128)[:, None]
    q = np.arange(128)[None, :]
    tri = np.zeros((128, 2, 3, 128), np.float32)
    tri[:, 0] = np.where(q <= k, 0.0, NEG)[:, None, :]
    tri[:, 1] = np.where(k <= q, 0.0, NEG)[:, None, :]
    cs["tri"] = tri.reshape(128, 2, 384)
    for L, tag in ((NS, "s"), (256, "p")):
        C, S = _dft_tables(L)
        cs["ctab_" + tag] = C
        cs["stab_" + tag] = S
        n = L // 128
        tt = np.arange(L)
        alt = np.where(tt % 2 == 0, 1.0, -1.0)
        cs["altc_" + tag] = _bf(alt.reshape(n, 128).T)
        cs["altr_" + tag] = _bf(alt.reshape(1, L))
        tf = (np.arange(L, dtype=np.float32) / np.float32(L)).astype(np.float32)
        bands = (2.0 * np.pi * np.arange(1, 17, dtype=np.float32)).astype(np.float32)
        a = tf[:, None] * bands[None, :]
        feats = np.concatenate([tf[:, None], np.sin(a), np.cos(a)], axis=-1).astype(np.float32)
        cs["feats_" + tag] = np.ascontiguousarray(feats.T)
        cs["negt_" + tag] = np.ascontiguousarray((-tf).reshape(n, 128).T)
        sc = np.full((128, 1), 2.0 / (2 * L), np.float32)
        sc[0, 0] = 1.0 / (2 * L)
        cs["sc0_" + tag] = sc
    return cs


def _na_tables(rpb):
    def rstart(r):
        return min(max(r - 4, 0), 56)

    def cstart(w):
        return min(max(w - 8, 0), 48)

    def table(j, lo, nch):
        p = np.arange(128)
        q = np.arange(128)
        out = np.full((128, 12, nch, 128), NEG, np.float32)
        r = 2 * j + q // 64
        w = q % 64
        rs = np.array([rstart(x) for x in r])
        cst = np.array([cstart(x) for x in w])
        for ci in range(nch):
            c = lo + ci
            kr = 2 * c + p // 64
            col = p % 64
            valid = ((kr[:, None] >= rs[None, :]) & (kr[:, None] < rs[None, :] + 8)
                     & (col[:, None] >= cst[None, :]) & (col[:, None] < cst[None, :] + 16))
            drow = np.clip(kr[:, None] - r[None, :] + 7, 0, 14)
            dcol = np.clip(col[:, None] - w[None, :] + 15, 0, 30)
            g = rpb[:, drow, dcol]
            out[:, :, ci, :] = np.where(valid[:, None, :], g.transpose(1, 0, 2), NEG)
        return out
    interior = table(2, 0, 5)
    edge = np.stack([table(0, 0, 4), table(1, 0, 4), table(30, 28, 4), table(31, 28, 4)], axis=0)
    return interior, edge


DEBUG = {}
SEGS = [("aq", 0, 3), ("ak", 3, 4), ("av", 4, 5), ("ag", 5, 8), ("bu", 8, 14), ("bg", 14, 16),
        ("cq", 16, 19), ("ck", 19, 22), ("cv", 22, 25), ("cg", 25, 28), ("mg", 28, 52)]


def seg_of(wb):
    for nm, a, b in SEGS:
        if a <= wb < b:
            return nm, wb - a
    raise ValueError


def build(dbg=False, layers=(0, 1), phases=("prep", "filt", "proj", "attn", "hy", "merge")):
    nc = bass.Bass("TRN2", target_bir_lowering=False)
    P = Prog(nc)

    def din(name, shape, dt=F32):
        return nc.dram_tensor(name, list(shape), dt, kind="ExternalInput").ap()

    def dout(name, shape, dt=F32):
        return nc.dram_tensor(name, list(shape), dt, kind="ExternalOutput").ap()

    def dscr(name, shape, dt=BF16):
        k = "ExternalOutput" if (dbg and name in DBG_NAMES) else "Internal"
        return nc.dram_tensor(name, list(shape), dt, kind=k).ap()

    xin = din("xin", [NT, D])
    c_lat = din("c_lat", [D])
    c_ctx = din("c_ctx", [D])
    cak = din("cak", [2, 512, 256]); cav = din("cav", [2, 512, 256])
    cck = din("cck", [2, 512, 768]); ccv = din("ccv", [2, 512, 768])
    norm_w = din("norm_w", [2, D]); w_ada = din("w_ada", [2, D, 3 * D]); b_ada = din("b_ada", [2, 3 * D])
    w_in = din("w_in", [2, D, INW]); a_sink = din("a_sink", [2, 12])
    hy_conv_w = din("hy_conv_w", [2, 3, 1536]); hy_conv_b = din("hy_conv_b", [2, 1536])
    hy_w1 = din("hy_w1", [2, 33, 64]); hy_b1 = din("hy_b1", [2, 64]); hy_w2 = din("hy_w2", [2, 64, 64])
    hy_b2 = din("hy_b2", [2, 64]); hy_freq = din("hy_freq", [2, 2, 64]); hy_w3 = din("hy_w3", [2, 64, 2048])
    hy_decay = din("hy_decay", [2, 2048]); hy_skip = din("hy_skip", [2, 2, 512])
    nab_i = din("nab_i", [2, 128, 12 * 5 * 128]); nab_e = din("nab_e", [2, 4, 128, 12 * 4 * 128])
    w_up_a = din("w_up_a", [2, 768, D]); w_up_b = din("w_up_b", [2, 512, D]); w_up_c = din("w_up_c", [2, 768, D])
    w_out = din("w_out", [2, D, D]); final_norm_w = din("final_norm_w", [D])
    ident_d = din("ident", [128, 128], BF16); identf_d = din("identf", [128, 128]); perm_d = din("perm", [128, 128], BF16)
    ropeC_d = din("ropeC", [128, NS]); ropeS_d = din("ropeS", [128, NS]); tri_d = din("tri", [128, 2, 384])
    tabs = {}
    for tag, L in (("s", NS), ("p", 256)):
        n = L // 128
        tabs[tag] = dict(L=L, n=n,
                         ctab=din("ctab_" + tag, [n, 128, n, 128], BF16), stab=din("stab_" + tag, [n, 128, n, 128], BF16),
                         altc=din("altc_" + tag, [128, n], BF16), altr=din("altr_" + tag, [1, L], BF16),
                         feats=din("feats_" + tag, [33, L]), negt=din("negt_" + tag, [128, n]), sc0=din("sc0_" + tag, [128, 1]))
    y_out = dout("y", [NT, D])
    nak = dout("nak", [4 * 2 * 256, 256]); nav = dout("nav", [4 * 2 * 256, 256])
    nck = dout("nck", [4 * 2 * 256, 768]); ncv = dout("ncv", [4 * 2 * 256, 768])
    DBG_NAMES = set(dbg) if dbg else set()
    w_in_b = dscr("w_in_b", [2, D, INW]); w_upa_b = dscr("w_upa_b", [2, 768, D]); w_upb_b = dscr("w_upb_b", [2, 512, D])
    w_upc_b = dscr("w_upc_b", [2, 768, D]); w_out_b = dscr("w_out_b", [2, D, D])
    modv = dscr("modv", [2, 2, 3, D], F32)
    qaT = dscr("qaT", [768, NT]); kaT = dscr("kaT", [256, NT]); va = dscr("va", [NT, 260]); ag = dscr("ag", [NT, 768])
    bu = dscr("bu", [NT, 1536]); bg = dscr("bg", [NT, 512]); qcT = dscr("qcT", [768, NT]); kcT = dscr("kcT", [768, NT])
    vc = dscr("vc", [NT, 780]); cg = dscr("cg", [NT, 768]); mgT = dscr("mgT", [6144, NT])
    yaT = dscr("yaT", [768, NT]); ybT = dscr("ybT", [512, NT]); ycT = dscr("ycT", [768, NT])
    x1c = dscr("x1c", [NT, 512]); x2c = dscr("x2c", [NT, 512])
    xres = dscr("xres", [NT, D], F32)
    spec = {tag: dscr("spec_" + tag, [2, 2, 2, tabs[tag]["L"], 512], F32) for tag in ("s", "p")}
    specn = {tag: dscr("specn_" + tag, [2, 2, 512], F32) for tag in ("s", "p")}
    R = {k: Res() for k in ["wb", "modv", "proj", "mix", "xres", "spec", "x12"]}

    def tile(st, shape, dt, pfx="t"):
        return Tl(st.enter_context(nc.sbuf_tensor(_name(pfx), list(shape), dt)))

    def ptile(st, shape, dt=F32, pfx="p"):
        return Tl(st.enter_context(nc.psum_tensor(_name(pfx), list(shape), dt)))

    def finish():
        P.barrier()
        P.emit()

    def phase_cast():
        for l in layers:
            for (src, dst, rows, cols) in ((w_in, w_in_b, D, INW), (w_up_a, w_upa_b, 768, D), (w_up_b, w_upb_b, 512, D),
                                           (w_up_c, w_upc_b, 768, D), (w_out, w_out_b, D, D)):
                step = 128 if cols > 4096 else 256
                for r0 in range(0, rows, step):
                    P.dma("pool", dst[l, r0:r0 + step, :], src[l, r0:r0 + step, :], w=[R["wb"]], max_dma_last_dim=4096)

    def phase_mod():
        with ExitStack() as st:
            idf = tile(st, [128, 128], F32)
            cc = tile(st, [16, 2, 128], F32)
            scT = tile(st, [128, 2, 16], F32)
            pT = ptile(st, [128, 2, 16], F32)
            wt = [tile(st, [128, 16, 512], F32) for _ in range(2)]
            pm = [ptile(st, [128, 512]) for _ in range(2)]
            bb = tile(st, [2, 3 * D], F32)
            nw2 = tile(st, [2, D], F32)
            mod = tile(st, [2, 3 * D], F32)
            P.dma("sp", idf.h[:], identf_d, w=[idf])
            P.dma("sp", cc.h[:, 0, :], c_ctx.rearrange("(k p) -> k p", p=128), w=[cc])
            P.dma("sp", cc.h[:, 1, :], c_lat.rearrange("(k p) -> k p", p=128), w=[cc])
            P.op("pe", [lambda e, g=g: e.transpose(out=pT.h[:, g, :], in_=cc.h[:, g, :], identity=idf.h[0:16, 0:16]) for g in range(2)],
                 r=[cc, idf], w=[pT])
            P.op("act", lambda e: e.activation(out=scT.h[:], in_=pT.h[:], func=AF.Silu), r=[pT], w=[scT])
            for l in layers:
                P.dma("sp", bb.h[:], b_ada[l].partition_broadcast(2), w=[bb])
                P.dma("sp", nw2.h[:], norm_w[l].partition_broadcast(2), w=[nw2])
                for cb in range(12):
                    w_t = wt[cb % 2]
                    p_t = pm[cb % 2]
                    P.dma("sp", w_t.h[:], w_ada[l, :, cb * 512:(cb + 1) * 512].rearrange("(kc p) n -> p kc n", p=128), w=[w_t])
                    P.op("pe", [lambda e, kc=kc, w_t=w_t, p_t=p_t: e.matmul(p_t.h[0:2, :], lhsT=scT.h[:, :, kc], rhs=w_t.h[:, kc, :],
                                                                           start=(kc == 0), stop=(kc == 15)) for kc in range(16)],
                         r=[scT, w_t], w=[p_t])
                    P.op("dve", lambda e, cb=cb, p_t=p_t: e.tensor_tensor(out=mod.h[:, cb * 512:(cb + 1) * 512], in0=p_t.h[0:2, :],
                                                                         in1=bb.h[:, cb * 512:(cb + 1) * 512], op=ALU.add),
                         r=[p_t, bb], w=[mod])
                P.op("dve", lambda e: e.scalar_tensor_tensor(out=mod.h[:, D:2 * D], in0=mod.h[:, D:2 * D], scalar=1.0, in1=nw2.h[:],
                                                             op0=ALU.add, op1=ALU.mult), r=[mod, nw2], w=[mod])
                for g in range(2):
                    for j, (a, b) in enumerate(((D, 2 * D), (0, D), (2 * D, 3 * D))):
                        P.dma("sp", modv[l, g, j, :].rearrange("(o d) -> o d", o=1), mod.h[g:g + 1, a:b], r=[mod], w=[R["modv"]])
        finish()

    def load_tab(tb, oc, ct, stt):
        P.dma("sp", ct.h[:], tb["ctab"][oc], w=[ct])
        P.dma("sp", stt.h[:], tb["stab"][oc], w=[stt])

    def phase_filt(tag):
        tb = tabs[tag]
        L, n = tb["L"], tb["n"]
        N = 2 * L
        nch = max(1, L // 512)
        cw = min(L, 512)
        TWO_PI = float(2 * np.pi)
        with ExitStack() as st:
            feats = tile(st, [33, L], F32)
            negt = tile(st, [128, n], F32)
            sc0 = tile(st, [128, 1], F32)
            altc = tile(st, [128, n], BF16)
            z1 = tile(st, [64, L], F32)
            z2 = tile(st, [64, L], F32)
            xa = tile(st, [64, 512], F32)
            mk = tile(st, [64, 512], F32)
            w1 = tile(st, [33, 64], F32); w2 = tile(st, [64, 64], F32); w3 = tile(st, [64, 2048], F32)
            fr = tile(st, [64, 2], F32); b1 = tile(st, [64, 1], F32); b2 = tile(st, [64, 1], F32)
            fb = tile(st, [64, 2], F32)
            dec = tile(st, [128, 1024], F32)
            E = tile(st, [128, 1024], F32)
            hf = tile(st, [128, 1024], F32)
            fbp = tile(st, [128, n, 512], BF16); fbm = tile(st, [128, n, 512], BF16)
            ct = [tile(st, [128, n, 128], BF16) for _ in range(2)]
            stt = [tile(st, [128, n, 128], BF16) for _ in range(2)]
            osr = [tile(st, [128, 512], F32) for _ in range(2)]
            osi = [tile(st, [128, 512], F32) for _ in range(2)]
            osn = tile(st, [1, 512], F32)
            pz = [ptile(st, [128, 512]) for _ in range(2)]
            ph = [ptile(st, [128, 512]) for _ in range(2)]
            pr = [ptile(st, [128, 512]) for _ in range(2)]
            pi = [ptile(st, [128, 512]) for _ in range(2)]
            P.dma("sp", feats.h[:], tb["feats"], w=[feats])
            P.dma("sp", negt.h[:], tb["negt"], w=[negt])
            P.dma("sp", sc0.h[:], tb["sc0"], w=[sc0])
            P.dma("sp", altc.h[:], tb["altc"], w=[altc])
            for l in layers:
                P.dma("sp", w1.h[:], hy_w1[l], w=[w1]); P.dma("sp", w2.h[:], hy_w2[l], w=[w2]); P.dma("sp", w3.h[:], hy_w3[l], w=[w3])
                for o in range(2):
                    P.dma("sp", fr.h[:, o:o + 1], hy_freq[l, o].rearrange("(p o) -> p o", o=1), w=[fr])
                P.dma("sp", b1.h[:], hy_b1[l].rearrange("(p o) -> p o", o=1), w=[b1])
                P.dma("sp", b2.h[:], hy_b2[l].rearrange("(p o) -> p o", o=1), w=[b2])
                P.op("dve", lambda e: e.tensor_tensor(out=fb.h[:, 0:1], in0=fr.h[:, 0:1], in1=b1.h[:], op=ALU.mult), r=[fr, b1], w=[fb])
                P.op("dve", lambda e: e.tensor_tensor(out=fb.h[:, 1:2], in0=fr.h[:, 1:2], in1=b2.h[:], op=ALU.mult), r=[fr, b2], w=[fb])
                for li, (wl, K, src, dst) in enumerate(((w1, 33, feats, z1), (w2, 64, z1, z2))):
                    for ch in range(nch):
                        p_t = pz[ch % 2]
                        P.op("pe", lambda e, p_t=p_t, wl=wl, K=K, src=src, ch=ch: e.matmul(p_t.h[0:64, 0:cw], lhsT=wl.h[0:K, :],
                                                                                            rhs=src.h[0:K, ch * cw:(ch + 1) * cw], start=True, stop=True),
                             r=[wl, src], w=[p_t])
                        P.op("dve", lambda e, p_t=p_t, li=li: e.tensor_scalar(out=xa.h[:, 0:cw], in0=p_t.h[0:64, 0:cw], scalar1=fr.h[:, li:li + 1],
                                                                              scalar2=fb.h[:, li:li + 1], op0=ALU.mult, op1=ALU.add),
                             r=[p_t, fr, fb], w=[xa])
                        for (thr, cmp_, add) in ((float(np.pi), ALU.is_gt, -TWO_PI), (-float(np.pi), ALU.is_lt, TWO_PI)) * 2:
                            P.op("dve", lambda e, thr=thr, cmp_=cmp_: e.tensor_single_scalar(out=mk.h[:, 0:cw], in_=xa.h[:, 0:cw], scalar=thr, op=cmp_),
                                 r=[xa], w=[mk])
                            P.op("dve", lambda e, add=add: e.scalar_tensor_tensor(out=xa.h[:, 0:cw], in0=mk.h[:, 0:cw], scalar=add, in1=xa.h[:, 0:cw],
                                                                                  op0=ALU.mult, op1=ALU.add), r=[mk, xa], w=[xa])
                        P.op("act", lambda e, dst=dst, ch=ch: e.activation(out=dst.h[:, ch * cw:(ch + 1) * cw], in_=xa.h[:, 0:cw], func=AF.Sin),
                             r=[xa], w=[dst])
                for o in range(2):
                    P.dma("sp", dec.h[:], hy_decay[l, o * 1024:(o + 1) * 1024].partition_broadcast(128), w=[dec])
                    P.op("act", lambda e: e.activation(out=dec.h[:], in_=dec.h[:], func=AF.Abs), r=[dec], w=[dec])
                    for i in range(n):
                        P.op("act", lambda e, i=i: e.activation(out=E.h[:], in_=dec.h[:], func=AF.Exp, scale=negt.h[:, i:i + 1]),
                             r=[dec, negt], w=[E])
                        for dr in range(2):
                            p_t = ph[dr]
                            P.op("pe", lambda e, p_t=p_t, i=i, dr=dr, o=o: e.matmul(p_t.h[:], lhsT=z2.h[0:64, i * 128:(i + 1) * 128],
                                                                                   rhs=w3.h[0:64, o * 1024 + dr * 512:o * 1024 + (dr + 1) * 512],
                                                                                   start=True, stop=True), r=[z2, w3], w=[p_t])
                            P.op("dve", lambda e, p_t=p_t, dr=dr: e.tensor_tensor(out=hf.h[:, dr * 512:(dr + 1) * 512], in0=p_t.h[:],
                                                                                 in1=E.h[:, dr * 512:(dr + 1) * 512], op=ALU.mult),
                                 r=[p_t, E], w=[hf])
                        P.op("dve", lambda e, i=i: e.tensor_tensor(out=fbp.h[:, i, :], in0=hf.h[:, 0:512], in1=hf.h[:, 512:1024], op=ALU.add),
                             r=[hf], w=[fbp])
                        P.op("dve", lambda e, i=i: e.tensor_tensor(out=fbm.h[:, i, :], in0=hf.h[:, 512:1024], in1=hf.h[:, 0:512], op=ALU.subtract),
                             r=[hf], w=[fbm])
                        if i == 0:
                            P.op("dve", lambda e: e.tensor_copy(out=fbp.h[0:1, 0, :], in_=hf.h[0:1, 0:512]), r=[hf], w=[fbp])
                    for j in range(n):
                        c_t, s_t = ct[j % 2], stt[j % 2]
                        load_tab(tb, j, c_t, s_t)
                        p_r, p_i = pr[j % 2], pi[j % 2]
                        P.op("pe", [lambda e, cc=cc, p_r=p_r, c_t=c_t: e.matmul(p_r.h[:], lhsT=c_t.h[:, cc, :], rhs=fbp.h[:, cc, :],
                                                                             start=(cc == 0), stop=(cc == n - 1)) for cc in range(n)],
                             r=[c_t, fbp], w=[p_r])
                        P.op("pe", [lambda e, cc=cc, p_i=p_i, s_t=s_t: e.matmul(p_i.h[:], lhsT=s_t.h[:, cc, :], rhs=fbm.h[:, cc, :],
                                                                             start=(cc == 0), stop=(cc == n - 1)) for cc in range(n)],
                             r=[s_t, fbm], w=[p_i])
                        o_r, o_i = osr[j % 2], osi[j % 2]
                        if j == 0:
                            P.op("act", lambda e, o_r=o_r, p_r=p_r: e.activation(out=o_r.h[:], in_=p_r.h[:], func=AF.Copy, scale=sc0.h[:, 0:1]),
                                 r=[p_r, sc0], w=[o_r])
                        else:
                            P.op("act", lambda e, o_r=o_r, p_r=p_r: e.activation(out=o_r.h[:], in_=p_r.h[:], func=AF.Copy, scale=2.0 / N),
                                 r=[p_r], w=[o_r])
                        P.op("act", lambda e, o_i=o_i, p_i=p_i: e.activation(out=o_i.h[:], in_=p_i.h[:], func=AF.Copy, scale=2.0 / N),
                             r=[p_i], w=[o_i])
                        P.dma("pool", spec[tag][l, o, 0, j * 128:(j + 1) * 128, :], o_r.h[:], r=[o_r], w=[R["spec"]])
                        P.dma("pool", spec[tag][l, o, 1, j * 128:(j + 1) * 128, :], o_i.h[:], r=[o_i], w=[R["spec"]])
                    p_r = pr[0]
                    P.op("pe", [lambda e, cc=cc, p_r=p_r: e.matmul(p_r.h[0:1, :], lhsT=altc.h[:, cc:cc + 1], rhs=fbp.h[:, cc, :],
                                                                 start=(cc == 0), stop=(cc == n - 1)) for cc in range(n)],
                         r=[altc, fbp], w=[p_r])
                    P.op("act", lambda e, p_r=p_r: e.activation(out=osn.h[:], in_=p_r.h[0:1, :], func=AF.Copy, scale=1.0 / N), r=[p_r], w=[osn])
                    P.dma("pool", specn[tag][l, o, :].rearrange("(o d) -> o d", o=1), osn.h[:], r=[osn], w=[R["spec"]])
        finish()

    def phase_proj(l):
        xsrc = xin if l == 0 else xres
        with ExitStack() as st:
            idb = tile(st, [128, 128], BF16); perm = tile(st, [128, 128], BF16)
            nw = tile(st, [128, D], F32); sh = tile(st, [128, D], F32)
            xt = [tile(st, [128, D], F32) for _ in range(2)]
            junk = tile(st, [128, D], BF16)
            ss = [tile(st, [128, 1], F32) for _ in range(2)]
            tmp = tile(st, [128, D], F32)
            hb = [tile(st, [128, D], BF16) for _ in range(2)]
            hT = [tile(st, [128, 16, 512], BF16) for _ in range(2)]
            wt = [tile(st, [128, 16, 512], BF16) for _ in range(4)]
            stF = [tile(st, [128, 2, 512], BF16) for _ in range(2)]
            stT = [tile(st, [128, 4, 256], BF16) for _ in range(2)]
            stV = [tile(st, [128, 4, 4, 65], BF16) for _ in range(2)]
            stO = [tile(st, [128, 4, 256], F32) for _ in range(2)]
            ub = tile(st, [128, 512], BF16); t1 = tile(st, [128, 512], F32); t2 = tile(st, [128, 512], F32)
            rC = tile(st, [128, 512], F32); rS = tile(st, [128, 512], F32)
            pm = [ptile(st, [128, 512]) for _ in range(4)]
            pT = ptile(st, [128, 16, 128], BF16)
            psw = ptile(st, [128, 512])
            P.dma("sp", idb.h[:], ident_d, w=[idb]); P.dma("sp", perm.h[:], perm_d, w=[perm])
            for v in stV:
                P.op("pool", lambda e, v=v: e.memset(v.h[:], 1.0), w=[v])
            cntF = cntT = cntV = cntO = 0
            pmi = 0
            nsub = 0
            nsubc = [0]

            def norm_stage(tt):
                g = 0 if tt < 8 else 1
                if tt in (0, 8):
                    P.dma("sp", nw.h[:], modv[l, 1 - g, 0, :].partition_broadcast(128), r=[R["modv"]], w=[nw])
                    P.dma("sp", sh.h[:], modv[l, 1 - g, 1, :].partition_broadcast(128), r=[R["modv"]], w=[sh])
                hTt = hT[tt % 2]
                for s in range(4):
                    nsub = nsubc[0]
                    tok0 = tt * 512 + s * 128
                    x_t = xt[nsub % 2]; s_t = ss[nsub % 2]; h_t = hb[nsub % 2]
                    nsubc[0] += 1
                    P.dma("sp", x_t.h[:], xsrc[tok0:tok0 + 128, :], r=[R["xres"]], w=[x_t])
                    P.op("act", lambda e, x_t=x_t, s_t=s_t: e.activation(out=junk.h[:], in_=x_t.h[:], func=AF.Square, accum_out=s_t.h[:]),
                         r=[x_t], w=[junk, s_t])
                    P.op("act", lambda e, s_t=s_t: e.activation(out=s_t.h[:], in_=s_t.h[:], func=AF.Sqrt, scale=1.0 / D, bias=EPS), r=[s_t], w=[s_t])
                    P.op("dve", lambda e, s_t=s_t: e.reciprocal(out=s_t.h[:], in_=s_t.h[:]), r=[s_t], w=[s_t])
                    P.op("dve", lambda e, x_t=x_t, s_t=s_t: e.scalar_tensor_tensor(out=tmp.h[:], in0=x_t.h[:], scalar=s_t.h[:, 0:1], in1=nw.h[:],
                                                                                   op0=ALU.mult, op1=ALU.mult), r=[x_t, s_t, nw], w=[tmp])
                    P.op("dve", lambda e, h_t=h_t: e.tensor_tensor(out=h_t.h[:], in0=tmp.h[:], in1=sh.h[:], op=ALU.add), r=[tmp, sh], w=[h_t])
                    P.op("pe", [lambda e, k=k, h_t=h_t: e.transpose(out=pT.h[:, k, :], in_=h_t.h[:, k * 128:(k + 1) * 128], identity=idb.h[:])
                                for k in range(16)], r=[h_t, idb], w=[pT])
                    P.op("act", lambda e, s=s, hTt=hTt: e.activation(out=hTt.h[:, :, s * 128:(s + 1) * 128], in_=pT.h[:], func=AF.Copy),
                         r=[pT], w=[hTt])

            tts = list(DEBUG.get('tts', range(NT // 512)))
            norm_stage(tts[0])
            for ti, tt in enumerate(tts):
                g = 0 if tt < 8 else 1
                sample = (g == 0)
                if sample:
                    P.dma("sp", rC.h[:], ropeC_d[:, tt * 512:(tt + 1) * 512], w=[rC])
                    P.dma("sp", rS.h[:], ropeS_d[:, tt * 512:(tt + 1) * 512], w=[rS])
                hTt = hT[tt % 2]
                for wb in DEBUG.get('wbs', range(52)):
                    if wb == 24 and ti + 1 < len(tts):
                        norm_stage(tts[ti + 1])
                    seg, bi = seg_of(wb)
                    w_t = wt[(wb // 2) % 4]
                    wo_ = (wb % 2) * 256
                    if wb % 2 == 0 or wb == DEBUG.get('wbs', [0])[0]:
                        wb0 = (wb // 2) * 512
                        P.dma("sp", w_t.h[:], w_in_b[l, :, wb0:wb0 + 512].rearrange("(kc p) n -> p kc n", p=128), r=[R["wb"]], w=[w_t])
                    isF = seg in ("aq", "ak", "cq", "ck", "mg")
                    if isF:
                        s_t = stF[cntF % 2]; cntF += 1
                        for half in range(2):
                            p_t = pm[pmi % 4]; pmi += 1
                            P.op("pe", [lambda e, kc=kc, half=half, p_t=p_t, w_t=w_t, wo_=wo_: e.matmul(p_t.h[:], lhsT=w_t.h[:, kc, wo_ + half * 128:wo_ + (half + 1) * 128],
                                                                                             rhs=hTt.h[:, kc, :], start=(kc == 0), stop=(kc == 15))
                                        for kc in range(16)], r=[w_t, hTt], w=[p_t])
                            if seg == "mg":
                                P.op("act", lambda e, half=half, p_t=p_t, s_t=s_t: e.activation(out=s_t.h[:, half, :], in_=p_t.h[:], func=AF.Sigmoid),
                                     r=[p_t], w=[s_t])
                            elif sample and seg in ("aq", "ak"):
                                sc = 0.125 if seg == "aq" else 1.0
                                P.op("act", lambda e, p_t=p_t, sc=sc: e.activation(out=ub.h[:], in_=p_t.h[:], func=AF.Copy, scale=sc), r=[p_t], w=[ub])
                                P.op("pe", lambda e: e.matmul(psw.h[:], lhsT=perm.h[:], rhs=ub.h[:], start=True, stop=True), r=[perm, ub], w=[psw])
                                P.op("dve", lambda e: e.tensor_tensor(out=t1.h[:], in0=ub.h[:], in1=rC.h[:], op=ALU.mult), r=[ub, rC], w=[t1])
                                P.op("dve", lambda e: e.tensor_tensor(out=t2.h[:], in0=psw.h[:], in1=rS.h[:], op=ALU.mult), r=[psw, rS], w=[t2])
                                P.op("dve", lambda e, half=half, s_t=s_t: e.tensor_tensor(out=s_t.h[:, half, :], in0=t1.h[:], in1=t2.h[:], op=ALU.add),
                                     r=[t1, t2], w=[s_t])
                            else:
                                sc = 0.125 if seg in ("aq", "cq") else 1.0
                                P.op("act", lambda e, half=half, p_t=p_t, s_t=s_t, sc=sc: e.activation(out=s_t.h[:, half, :], in_=p_t.h[:], func=AF.Copy, scale=sc),
                                     r=[p_t], w=[s_t])
                        dst = {"aq": qaT, "ak": kaT, "cq": qcT, "ck": kcT, "mg": mgT}[seg]
                        P.dma("pool", dst[bi * 256:(bi + 1) * 256, tt * 512:(tt + 1) * 512].rearrange("(h p) t -> p h t", p=128), s_t.h[:],
                              r=[s_t], w=[R["proj"]])
                    needT = (not isF) or ((not sample) and seg in ("ak", "ck") and not DEBUG.get('noO'))
                    if needT:
                        isV = seg in ("av", "cv")
                        wantO = (not sample) and seg in ("ak", "av", "ck", "cv") and not DEBUG.get('noO')
                        if isV:
                            s_t = stV[cntV % 2]; cntV += 1
                        elif not isF:
                            s_t = stT[cntT % 2]; cntT += 1
                        if wantO:
                            o_t = stO[cntO % 2]; cntO += 1
                        for s in range(4):
                            p_t = pm[pmi % 4]; pmi += 1
                            P.op("pe", [lambda e, kc=kc, s=s, p_t=p_t, w_t=w_t: e.matmul(p_t.h[:, 0:256], lhsT=hTt.h[:, kc, s * 128:(s + 1) * 128],
                                                                                        rhs=w_t.h[:, kc, wo_:wo_ + 256], start=(kc == 0), stop=(kc == 15))
                                        for kc in range(16)], r=[w_t, hTt], w=[p_t])
                            if wantO:
                                P.op("act", lambda e, s=s, p_t=p_t, o_t=o_t: e.activation(out=o_t.h[:, s, :], in_=p_t.h[:, 0:256], func=AF.Copy), r=[p_t], w=[o_t])
                            if isV:
                                P.op("act", lambda e, s=s, p_t=p_t, s_t=s_t: e.activation(out=s_t.h[:, s, :, 0:64],
                                                                                          in_=p_t.h[:, 0:256].rearrange("p (h d) -> p h d", h=4),
                                                                                          func=AF.Copy), r=[p_t], w=[s_t])
                            elif not isF:
                                fn = AF.Silu if seg in ("ag", "bg", "cg") else AF.Copy
                                P.op("act", lambda e, s=s, p_t=p_t, s_t=s_t, fn=fn: e.activation(out=s_t.h[:, s, :], in_=p_t.h[:, 0:256], func=fn),
                                     r=[p_t], w=[s_t])
                        trange = slice(tt * 512, (tt + 1) * 512)
                        if isV:
                            dst = va if seg == "av" else vc
                            P.dma("pool", dst[trange, bi * 260:(bi + 1) * 260].rearrange("(s p) c -> p s c", p=128),
                                  s_t.h[:].rearrange("p s h d -> p s (h d)"), r=[s_t], w=[R["proj"]])
                        elif not isF:
                            dst = {"ag": ag, "bu": bu, "bg": bg, "cg": cg}[seg]
                            P.dma("pool", dst[trange, bi * 256:(bi + 1) * 256].rearrange("(s p) c -> p s c", p=128), s_t.h[:],
                                  r=[s_t], w=[R["proj"]])
                        if wantO:
                            dsto = {"ak": nak, "av": nav, "ck": nck, "cv": ncv}[seg]
                            for s in range(4):
                                sq = (tt - 8) * 2 + s // 2
                                p0 = (s % 2) * 128
                                r0 = (sq * 2 + l) * 256 + p0
                                P.dma("sp", dsto[r0:r0 + 128, bi * 256:(bi + 1) * 256], o_t.h[:, s, :], r=[o_t])
        finish()

    def phase_attn(l, which, sample, conv_args=None):
        isA = (which == "A")
        nkv = 4 if isA else 12
        gq = 3 if isA else 1
        qT_d, kT_d, v_d, g_d, y_d = (qaT, kaT, va, ag, yaT) if isA else (qcT, kcT, vc, cg, ycT)
        vw = nkv * 65
        maxch = (3 if isA else 5) if sample else 2
        with ExitStack() as st:
            idb = tile(st, [128, 128], BF16)
            P.dma("sp", idb.h[:], ident_d, w=[idb])
            Qt = [tile(st, [64, 12, 128], BF16) for _ in range(2)]
            Kt = [tile(st, [64, nkv, maxch * 128], BF16) for _ in range(2)]
            Vt = [tile(st, [128, maxch, vw], BF16) for _ in range(2)]
            Gt = [tile(st, [128, 768], BF16) for _ in range(2)]
            pt = [tile(st, [128, (3 if isA else 4) * 128], BF16) for _ in range(4)]
            dn = tile(st, [128, 12], F32)
            yn = tile(st, [128, 12, 64], F32)
            yg = tile(st, [128, 768], BF16)
            ys = [tile(st, [128, 6, 128], BF16) for _ in range(2)]
            pS = [ptile(st, [128, 512]) for _ in range(2)]
            pO = [[ptile(st, [128, 6, 65]) for _ in range(2)] for _ in range(2)]
            pY = ptile(st, [128, 6, 128], BF16)
            if sample:
                ck_d, cv_d = (cak, cav) if isA else (cck, ccv)
                kf = tile(st, [128, 4, nkv * 64], F32)
                kb = tile(st, [128, 4, nkv * 64], BF16)
                KTc = tile(st, [64, nkv, 512], BF16)
                Vc = tile(st, [128, 4, nkv, 65], BF16)
                P.dma("sp", kf.h[:], ck_d[l].rearrange("(c p) x -> p c x", p=128), w=[kf])
                P.op("dve", lambda e: e.tensor_copy(out=kb.h[:], in_=kf.h[:]), r=[kf], w=[kb])
                for h in range(nkv):
                    P.op("pe", [lambda e, c=c, h=h: e.transpose(out=pY.h[0:64, c, :], in_=kb.h[:, c, h * 64:(h + 1) * 64], identity=idb.h[:])
                                for c in range(4)], r=[kb, idb], w=[pY])
                    P.op("act", lambda e, h=h: e.activation(out=KTc.h[:, h, :].rearrange("p (c k) -> p c k", c=4), in_=pY.h[0:64, 0:4, :], func=AF.Copy),
                         r=[pY], w=[KTc])
                P.dma("sp", kf.h[:], cv_d[l].rearrange("(c p) x -> p c x", p=128), w=[kf])
                P.op("pool", lambda e: e.memset(Vc.h[:], 1.0), w=[Vc])
                for c in range(4):
                    P.op("dve", lambda e, c=c: e.tensor_copy(out=Vc.h[:, c, :, 0:64], in_=kf.h[:, c, :].rearrange("p (h d) -> p h d", h=nkv)),
                         r=[kf], w=[Vc])
                if isA:
                    trif = tile(st, [128, 2, 384], F32)
                    trib = tile(st, [128, 2, 384], BF16)
                    P.dma("sp", trif.h[:], tri_d, w=[trif])
                    P.op("dve", lambda e: e.tensor_copy(out=trib.h[:], in_=trif.h[:]), r=[trif], w=[trib])
                else:
                    nbi = tile(st, [128, 12, 5, 128], BF16)
                    nbe = tile(st, [128, 12, 4, 128], BF16)
                    for h0 in range(0, 12, 3):
                        P.dma("pool", nbi.h[:, h0:h0 + 3].rearrange("p h c q -> p (h c q)"), nab_i[l, :, h0 * 640:(h0 + 3) * 640], w=[nbi],
                              max_dma_last_dim=4096)
            if isA:
                esk = tile(st, [128, 12], F32)
                P.dma("sp", esk.h[:], a_sink[l].partition_broadcast(128), w=[esk])
                P.op("act", lambda e: e.activation(out=esk.h[:], in_=esk.h[:], func=AF.Exp), r=[esk], w=[esk])
            if sample:
                qtiles = [(i * 128,) for i in range(32)]
            else:
                qtiles = [(NS + 256 * s + 128 * a,) for s in range(4) for a in range(2)]
            hg = 3 if isA else 4
            ngrp = 12 // hg
            ctr = {"n": 0}
            conv = make_conv(l, conv_args[0], conv_args[1], st) if conv_args is not None else None

            def finalize(qi, q0, O, G):
                for b in range(2):
                    if isA:
                        P.op("dve", lambda e, b=b, O=O: e.tensor_tensor(out=dn.h[:, b * 6:(b + 1) * 6], in0=O[b].h[:, :, 64], in1=esk.h[:, b * 6:(b + 1) * 6],
                                                                        op=ALU.add), r=[O[b], esk], w=[dn])
                    else:
                        P.op("act", lambda e, b=b, O=O: e.activation(out=dn.h[:, b * 6:(b + 1) * 6], in_=O[b].h[:, :, 64], func=AF.Copy), r=[O[b]], w=[dn])
                P.op("dve", lambda e: e.reciprocal(out=dn.h[:], in_=dn.h[:]), r=[dn], w=[dn])
                for b in range(2):
                    P.op("dve", lambda e, b=b, O=O: e.tensor_tensor(out=yn.h[:, b * 6:(b + 1) * 6, :], in0=O[b].h[:, :, 0:64],
                                                                    in1=dn.h[:, b * 6:(b + 1) * 6].unsqueeze(2).to_broadcast([128, 6, 64]), op=ALU.mult),
                         r=[O[b], dn], w=[yn])
                P.op("dve", lambda e, G=G: e.tensor_tensor(out=yg.h[:], in0=yn.h[:].rearrange("p h d -> p (h d)"), in1=G.h[:], op=ALU.mult),
                     r=[yn, G], w=[yg])
                P.op("pe", [lambda e, k=k: e.transpose(out=pY.h[:, k, :], in_=yg.h[:, k * 128:(k + 1) * 128], identity=idb.h[:]) for k in range(6)],
                     r=[yg, idb], w=[pY])
                y_s = ys[qi % 2]
                P.op("act", lambda e, y_s=y_s: e.activation(out=y_s.h[:], in_=pY.h[:], func=AF.Copy), r=[pY], w=[y_s])
                P.dma("pool", y_d[:, q0:q0 + 128].rearrange("(c p) t -> p c t", p=128), y_s.h[:], r=[y_s], w=[R["mix"]])

            def steps():
                for qi, (q0,) in enumerate(qtiles):
                    interior = False
                    if sample and isA:
                        i = qi
                        lo, hi = max(0, i - 1), min(31, i + 1)
                        chunks = [("loc", c - lo, (0 if c == i - 1 else (1 if c == i + 1 else None))) for c in range(lo, hi + 1)]
                    elif sample:
                        j = qi
                        rs0 = min(max(2 * j - 4, 0), 56); rs1 = min(max(2 * j + 1 - 4, 0), 56)
                        lo, hi = rs0 // 2, (rs1 + 7) // 2
                        interior = 2 <= j <= 29
                        et = {0: 0, 1: 1, 30: 2, 31: 3}.get(j)
                        chunks = [("loc", c - lo, ((c - j + 2) if interior else (c - lo))) for c in range(lo, hi + 1)]
                        if not interior:
                            for h0 in range(0, 12, 3):
                                P.dma("pool", nbe.h[:, h0:h0 + 3].rearrange("p h c q -> p (h c q)"), nab_e[l, et, :, h0 * 512:(h0 + 3) * 512], w=[nbe],
                                      max_dma_last_dim=4096)
                    else:
                        sq0 = q0 - ((q0 - NS) % 256)
                        lo, hi = sq0 // 128, sq0 // 128 + 1
                        chunks = [("loc", 0, None), ("loc", 1, None)]
                    if sample:
                        chunks = chunks + [("ctx", c, None) for c in range(4)]
                    nloc = hi - lo + 1
                    Q = Qt[qi % 2]; K = Kt[qi % 2]; V = Vt[qi % 2]; G = Gt[qi % 2]
                    P.dma("sp", Q.h[:], qT_d[:, q0:q0 + 128].rearrange("(h d) t -> d h t", d=64), r=[R["proj"]], w=[Q])
                    P.dma("sp", K.h[:, :, 0:nloc * 128], kT_d[:, lo * 128:(hi + 1) * 128].rearrange("(h d) t -> d h t", d=64), r=[R["proj"]], w=[K])
                    P.dma("sp", V.h[:, 0:nloc, :], v_d[lo * 128:(hi + 1) * 128, :].rearrange("(c p) x -> p c x", p=128), r=[R["proj"]], w=[V])
                    P.dma("sp", G.h[:], g_d[q0:q0 + 128, :], r=[R["proj"]], w=[G])
                    O = pO[qi % 2]
                    started = [False, False]
                    nst = ngrp * len(chunks)
                    k = 0
                    for g in range(ngrp):
                        heads = list(range(g * hg, (g + 1) * hg))
                        for (kind, cx, mask) in chunks:
                            k += 1
                            n = ctr["n"]; ctr["n"] += 1
                            p_s = pS[n % 2]; p_t = pt[n % 4]
                            Ksrc = K if kind == "loc" else KTc
                            Vsrc = V if kind == "loc" else Vc
                            sf = []
                            if isA:
                                sf.append(lambda e, p_s=p_s, Ksrc=Ksrc, g=g, cx=cx, Q=Q, mask=mask: e.matmul(
                                    p_s.h[:, 0:384], lhsT=Ksrc.h[:, g, cx * 128:(cx + 1) * 128], rhs=Q.h[:, 3 * g:3 * g + 3, :], start=True, stop=(mask is None)))
                            else:
                                for jh, hd in enumerate(heads):
                                    sf.append(lambda e, p_s=p_s, Ksrc=Ksrc, hd=hd, jh=jh, cx=cx, Q=Q, mask=mask: e.matmul(
                                        p_s.h[:, jh * 128:(jh + 1) * 128], lhsT=Ksrc.h[:, hd, cx * 128:(cx + 1) * 128], rhs=Q.h[:, hd, :],
                                        start=(jh == 0), stop=(mask is None and jh == hg - 1), skip_group_check=True))
                            rr = [Ksrc, Q]
                            if mask is not None:
                                if isA:
                                    m_ap = trib.h[:, mask, :]; rr.append(trib)
                                elif interior:
                                    m_ap = nbi.h[:, g * hg:(g + 1) * hg, mask, :]; rr.append(nbi)
                                else:
                                    m_ap = nbe.h[:, g * hg:(g + 1) * hg, mask, :]; rr.append(nbe)
                                rr.append(idb)
                                sf.append(lambda e, p_s=p_s, m_ap=m_ap: e.matmul(p_s.h[:, 0:hg * 128], lhsT=idb.h[:], rhs=m_ap, start=False, stop=True,
                                                                               skip_group_check=True))
                            pf = []
                            banks = []
                            for jh, hd in enumerate(heads):
                                ob = O[hd // 6]
                                if ob not in banks:
                                    banks.append(ob)
                                kvh = g if isA else hd
                                v_ap = Vsrc.h[:, cx, kvh * 65:(kvh + 1) * 65] if kind == "loc" else Vsrc.h[:, cx, kvh, :]
                                first = not started[hd // 6]
                                started[hd // 6] = True
                                pf.append(lambda e, ob=ob, hd=hd, jh=jh, p_t=p_t, v_ap=v_ap, first=first: e.matmul(
                                    ob.h[:, hd % 6, :], lhsT=p_t.h[:, jh * 128:(jh + 1) * 128], rhs=v_ap, start=first, stop=False, skip_group_check=True))
                            yield dict(sf=sf, rr=rr, p_s=p_s, p_t=p_t, pf=pf, pr=[p_t, Vsrc], banks=banks, last=(k == nst), fin=(qi, q0, O, G))

            it = steps()
            cur = next(it)
            P.op("pe", cur["sf"], r=cur["rr"], w=[cur["p_s"]])
            while cur is not None:
                nxt = next(it, None)
                if nxt is not None:
                    P.op("pe", nxt["sf"], r=nxt["rr"], w=[nxt["p_s"]])
                P.op("act", lambda e, p_s=cur["p_s"], p_t=cur["p_t"]: e.activation(out=p_t.h[:], in_=p_s.h[:, 0:hg * 128], func=AF.Exp),
                     r=[cur["p_s"]], w=[cur["p_t"]])
                P.op("pe", cur["pf"], r=cur["pr"], w=cur["banks"])
                if cur["last"]:
                    finalize(*cur["fin"])
                    if conv is not None:
                        conv(cur["fin"][0])
                cur = nxt
        finish()

    def make_conv(l, tag, VZ, st1):
        tb = tabs[tag]
        L, n = tb["L"], tb["n"]
        base = 0 if tag == "s" else NS
        cwb = tile(st1, [128, 4, 1536], F32)
        for j in range(3):
            P.dma("sp", cwb.h[:, j, :], hy_conv_w[l, j].partition_broadcast(128), w=[cwb])
        P.dma("sp", cwb.h[:, 3, :], hy_conv_b[l].partition_broadcast(128), w=[cwb])
        sh3 = [[tile(st1, [128, 1536], BF16) for _ in range(3)] for _ in range(2)]
        ca = tile(st1, [128, 1536], F32); cb_ = tile(st1, [128, 1536], F32)
        x12 = [tile(st1, [128, 2, 512], BF16) for _ in range(2)]

        def conv(k):
            sq, i = k // n, k % n
            t0 = base + sq * L + i * 128
            a, b, c = sh3[k % 2]
            if i == 0:
                P.op("pool", lambda e, a=a: e.memset(a.h[:], 0.0), w=[a])
                P.dma("sp", a.h[1:128, :], bu[t0:t0 + 127, :], r=[R["proj"]], w=[a])
            else:
                P.dma("sp", a.h[:], bu[t0 - 1:t0 + 127, :], r=[R["proj"]], w=[a])
            P.dma("sp", b.h[:], bu[t0:t0 + 128, :], r=[R["proj"]], w=[b])
            if i == n - 1:
                P.op("pool", lambda e, c=c: e.memset(c.h[:], 0.0), w=[c])
                P.dma("sp", c.h[0:127, :], bu[t0 + 1:t0 + 128, :], r=[R["proj"]], w=[c])
            else:
                P.dma("sp", c.h[:], bu[t0 + 1:t0 + 129, :], r=[R["proj"]], w=[c])
            P.op("dve", lambda e, a=a: e.tensor_tensor(out=ca.h[:], in0=a.h[:], in1=cwb.h[:, 0, :], op=ALU.mult), r=[a, cwb], w=[ca])
            P.op("dve", lambda e: e.tensor_tensor(out=ca.h[:], in0=ca.h[:], in1=cwb.h[:, 3, :], op=ALU.add), r=[ca, cwb], w=[ca])
            P.op("dve", lambda e, b=b: e.tensor_tensor(out=cb_.h[:], in0=b.h[:], in1=cwb.h[:, 1, :], op=ALU.mult), r=[b, cwb], w=[cb_])
            P.op("dve", lambda e: e.tensor_tensor(out=ca.h[:], in0=ca.h[:], in1=cb_.h[:], op=ALU.add), r=[ca, cb_], w=[ca])
            P.op("dve", lambda e, c=c: e.tensor_tensor(out=cb_.h[:], in0=c.h[:], in1=cwb.h[:, 2, :], op=ALU.mult), r=[c, cwb], w=[cb_])
            x_t = x12[k % 2]
            P.op("dve", lambda e, i=i, sq=sq: e.tensor_tensor(out=VZ.h[:, i, sq * 512:(sq + 1) * 512], in0=ca.h[:, 0:512], in1=cb_.h[:, 0:512], op=ALU.add),
                 r=[ca, cb_], w=[VZ])
            P.op("dve", lambda e, x_t=x_t: e.tensor_tensor(out=x_t.h[:].rearrange("p a c -> p (a c)"), in0=ca.h[:, 512:1536], in1=cb_.h[:, 512:1536], op=ALU.add),
                 r=[ca, cb_], w=[x_t])
            P.dma("pool", x1c[t0:t0 + 128, :], x_t.h[:, 0, :], r=[x_t], w=[R["x12"]])
            P.dma("pool", x2c[t0:t0 + 128, :], x_t.h[:, 1, :], r=[x_t], w=[R["x12"]])
        return conv

    def phase_hyena(l, tag, VZ):
        tb = tabs[tag]
        L, n = tb["L"], tb["n"]
        nseq = 1 if tag == "s" else 4
        base = 0 if tag == "s" else NS
        with ExitStack() as st:
            idb = tile(st, [128, 128], BF16)
            P.dma("sp", idb.h[:], ident_d, w=[idb])
            altc = tile(st, [128, n], BF16); altr = tile(st, [1, L], BF16)
            P.dma("sp", altc.h[:], tb["altc"], w=[altc]); P.dma("sp", altr.h[:], tb["altr"], w=[altr])
            skb = tile(st, [128, 2, 512], F32)
            for o in range(2):
                P.dma("sp", skb.h[:, o, :], hy_skip[l, o].partition_broadcast(128), w=[skb])
            spn = tile(st, [1, 2, 512], F32)
            P.dma("sp", spn.h[:], specn[tag][l].rearrange("(x o) c -> x o c", x=1), r=[R["spec"]], w=[spn])
            pr = [ptile(st, [128, 512]) for _ in range(2)]
            pi = [ptile(st, [128, 512]) for _ in range(2)]
            py = [ptile(st, [128, 512]) for _ in range(2)]
            pY = ptile(st, [128, 4, 128], BF16)
            Ya = tile(st, [128, n, nseq * 512], BF16); Yb = tile(st, [128, n, nseq * 512], BF16)
            Yn = tile(st, [1, nseq * 512], BF16)
            ct = [tile(st, [128, n, 128], BF16) for _ in range(2)]
            stt = [tile(st, [128, n, 128], BF16) for _ in range(2)]
            sr = [tile(st, [128, 512], F32) for _ in range(2)]; si = [tile(st, [128, 512], F32) for _ in range(2)]
            t1 = tile(st, [128, 512], F32); t2 = tile(st, [128, 512], F32)
            xin_t = [tile(st, [128, 512], BF16) for _ in range(2)]
            bgt = [tile(st, [128, 512], BF16) for _ in range(2)]
            yb_t = tile(st, [128, 512], BF16)
            ybs = [tile(st, [128, 4, 128], BF16) for _ in range(2)]
            for o in range(2):
                for j in range(n):
                    c_t, s_t = ct[j % 2], stt[j % 2]
                    load_tab(tb, j, c_t, s_t)
                    s_r, s_i = sr[j % 2], si[j % 2]
                    P.dma("sp", s_r.h[:], spec[tag][l, o, 0, j * 128:(j + 1) * 128, :], r=[R["spec"]], w=[s_r])
                    P.dma("sp", s_i.h[:], spec[tag][l, o, 1, j * 128:(j + 1) * 128, :], r=[R["spec"]], w=[s_i])
                    for sq in range(nseq):
                        cs = slice(sq * 512, (sq + 1) * 512)
                        p_r, p_i = pr[(j * nseq + sq) % 2], pi[(j * nseq + sq) % 2]
                        P.op("pe", [lambda e, cc=cc, p_r=p_r, c_t=c_t, cs=cs: e.matmul(p_r.h[:], lhsT=c_t.h[:, cc, :], rhs=VZ.h[:, cc, cs],
                                                                                    start=(cc == 0), stop=(cc == n - 1)) for cc in range(n)],
                             r=[c_t, VZ], w=[p_r])
                        P.op("pe", [lambda e, cc=cc, p_i=p_i, s_t=s_t, cs=cs: e.matmul(p_i.h[:], lhsT=s_t.h[:, cc, :], rhs=VZ.h[:, cc, cs],
                                                                                    start=(cc == 0), stop=(cc == n - 1)) for cc in range(n)],
                             r=[s_t, VZ], w=[p_i])
                        P.op("dve", lambda e, p_r=p_r, s_r=s_r: e.tensor_tensor(out=t1.h[:], in0=p_r.h[:], in1=s_r.h[:], op=ALU.mult), r=[p_r, s_r], w=[t1])
                        P.op("dve", lambda e, p_i=p_i, s_i=s_i: e.tensor_tensor(out=t2.h[:], in0=p_i.h[:], in1=s_i.h[:], op=ALU.mult), r=[p_i, s_i], w=[t2])
                        P.op("dve", lambda e, j=j, cs=cs: e.tensor_tensor(out=Ya.h[:, j, cs], in0=t1.h[:], in1=t2.h[:], op=ALU.add), r=[t1, t2], w=[Ya])
                        P.op("dve", lambda e, p_i=p_i, s_r=s_r: e.tensor_tensor(out=t1.h[:], in0=p_i.h[:], in1=s_r.h[:], op=ALU.mult), r=[p_i, s_r], w=[t1])
                        P.op("dve", lambda e, p_r=p_r, s_i=s_i: e.tensor_tensor(out=t2.h[:], in0=p_r.h[:], in1=s_i.h[:], op=ALU.mult), r=[p_r, s_i], w=[t2])
                        P.op("dve", lambda e, j=j, cs=cs: e.tensor_tensor(out=Yb.h[:, j, cs], in0=t1.h[:], in1=t2.h[:], op=ALU.subtract), r=[t1, t2], w=[Yb])
                for sq in range(nseq):
                    cs = slice(sq * 512, (sq + 1) * 512)
                    p_r = pr[sq % 2]
                    P.op("pe", [lambda e, cc=cc, p_r=p_r, cs=cs: e.matmul(p_r.h[0:1, :], lhsT=altc.h[:, cc:cc + 1], rhs=VZ.h[:, cc, cs],
                                                                        start=(cc == 0), stop=(cc == n - 1)) for cc in range(n)], r=[altc, VZ], w=[p_r])
                    P.op("dve", lambda e, p_r=p_r, cs=cs, o=o: e.tensor_tensor(out=Yn.h[:, cs], in0=p_r.h[0:1, :], in1=spn.h[:, o, :], op=ALU.mult),
                         r=[p_r, spn], w=[Yn])
                for i in range(n):
                    c_t, s_t = ct[(n + i) % 2], stt[(n + i) % 2]
                    load_tab(tb, i, c_t, s_t)
                    for sq in range(nseq):
                        cs = slice(sq * 512, (sq + 1) * 512)
                        t0 = base + sq * L + i * 128
                        p_y = py[(i * nseq + sq) % 2]
                        fns = [lambda e, cc=cc, p_y=p_y, c_t=c_t, cs=cs: e.matmul(p_y.h[:], lhsT=c_t.h[:, cc, :], rhs=Ya.h[:, cc, cs], start=(cc == 0), stop=False)
                               for cc in range(n)]
                        fns += [lambda e, cc=cc, p_y=p_y, s_t=s_t, cs=cs: e.matmul(p_y.h[:], lhsT=s_t.h[:, cc, :], rhs=Yb.h[:, cc, cs], start=False, stop=False)
                                for cc in range(n)]
                        fns += [lambda e, p_y=p_y, i=i, cs=cs: e.matmul(p_y.h[:], lhsT=altr.h[0:1, i * 128:(i + 1) * 128], rhs=Yn.h[0:1, cs], start=False, stop=True)]
                        P.op("pe", fns, r=[c_t, s_t, Ya, Yb, Yn, altr], w=[p_y])
                        x_t = xin_t[(i * nseq + sq) % 2]
                        P.dma("sp", x_t.h[:], (x1c if o == 0 else x2c)[t0:t0 + 128, :], r=[R["x12"]], w=[x_t])
                        P.op("dve", lambda e, i=i, cs=cs, o=o: e.tensor_tensor(out=t1.h[:], in0=VZ.h[:, i, cs], in1=skb.h[:, o, :], op=ALU.mult), r=[VZ, skb], w=[t1])
                        P.op("dve", lambda e, p_y=p_y: e.tensor_tensor(out=t1.h[:], in0=p_y.h[:], in1=t1.h[:], op=ALU.add), r=[p_y, t1], w=[t1])
                        if o == 0:
                            P.op("dve", lambda e, i=i, cs=cs, x_t=x_t: e.tensor_tensor(out=VZ.h[:, i, cs], in0=t1.h[:], in1=x_t.h[:], op=ALU.mult),
                                 r=[t1, x_t], w=[VZ])
                        else:
                            b_t = bgt[(i * nseq + sq) % 2]
                            P.dma("sp", b_t.h[:], bg[t0:t0 + 128, :], r=[R["proj"]], w=[b_t])
                            P.op("dve", lambda e, x_t=x_t: e.tensor_tensor(out=t2.h[:], in0=t1.h[:], in1=x_t.h[:], op=ALU.mult), r=[t1, x_t], w=[t2])
                            P.op("dve", lambda e, b_t=b_t: e.tensor_tensor(out=yb_t.h[:], in0=t2.h[:], in1=b_t.h[:], op=ALU.mult), r=[t2, b_t], w=[yb_t])
                            P.op("pe", [lambda e, k=k: e.transpose(out=pY.h[:, k, :], in_=yb_t.h[:, k * 128:(k + 1) * 128], identity=idb.h[:]) for k in range(4)],
                                 r=[yb_t, idb], w=[pY])
                            y_s = ybs[(i * nseq + sq) % 2]
                            P.op("act", lambda e, y_s=y_s: e.activation(out=y_s.h[:], in_=pY.h[:], func=AF.Copy), r=[pY], w=[y_s])
                            P.dma("pool", ybT[:, t0:t0 + 128].rearrange("(c p) t -> p c t", p=128), y_s.h[:], r=[y_s], w=[R["mix"]])
        finish()

    def phase_merge(l):
        last = (l == 1)
        xsrc = xin if l == 0 else xres
        with ExitStack() as st:
            wa = tile(st, [128, 6, D], BF16); wbt = tile(st, [128, 4, D], BF16); wc = tile(st, [128, 6, D], BF16)
            P.dma("sp", wa.h[:], w_upa_b[l].rearrange("(kc p) n -> p kc n", p=128), r=[R["wb"]], w=[wa])
            P.dma("sp", wbt.h[:], w_upb_b[l].rearrange("(kc p) n -> p kc n", p=128), r=[R["wb"]], w=[wbt])
            P.dma("sp", wc.h[:], w_upc_b[l].rearrange("(kc p) n -> p kc n", p=128), r=[R["wb"]], w=[wc])
            gate = tile(st, [128, D], F32)
            if last:
                fnw = tile(st, [128, D], F32)
                P.dma("sp", fnw.h[:], final_norm_w.partition_broadcast(128), w=[fnw])
            ya_t = tile(st, [128, 6, 512], BF16); yb_t = tile(st, [128, 4, 512], BF16); yc_t = tile(st, [128, 6, 512], BF16)
            mg_t = [tile(st, [128, 3, 512], BF16) for _ in range(2)]
            mT = tile(st, [128, 16, 512], BF16)
            wo = [tile(st, [128, 16, 256], BF16) for _ in range(2)]
            xt = [tile(st, [128, D], F32) for _ in range(4)]
            t1 = tile(st, [128, 512], F32); t2 = tile(st, [128, 512], F32)
            tq_ = [tile(st, [128, 256], F32) for _ in range(2)]
            junk = tile(st, [128, D], BF16)
            ss = tile(st, [128, 1], F32)
            pa2 = [ptile(st, [128, 512]) for _ in range(2)]; pb2 = [ptile(st, [128, 512]) for _ in range(2)]
            pc2 = [ptile(st, [128, 512]) for _ in range(2)]
            po = pa2
            tA = [tile(st, [128, 512], F32) for _ in range(2)]; tB = [tile(st, [128, 512], F32) for _ in range(2)]
            nwo = 0
            for tt in range(NT // 512):
                g = 0 if tt < 8 else 1
                if tt in (0, 8):
                    P.dma("sp", gate.h[:], modv[l, 1 - g, 2, :].partition_broadcast(128), r=[R["modv"]], w=[gate])
                ts_ = slice(tt * 512, (tt + 1) * 512)
                P.dma("sp", ya_t.h[:], yaT[:, ts_].rearrange("(c p) t -> p c t", p=128), r=[R["mix"]], w=[ya_t])
                P.dma("sp", yb_t.h[:], ybT[:, ts_].rearrange("(c p) t -> p c t", p=128), r=[R["mix"]], w=[yb_t])
                P.dma("sp", yc_t.h[:], ycT[:, ts_].rearrange("(c p) t -> p c t", p=128), r=[R["mix"]], w=[yc_t])
                for fo in range(16):
                    m_t = mg_t[fo % 2]
                    for b in range(3):
                        P.dma("sp", m_t.h[:, b, :], mgT[b * 2048 + fo * 128:b * 2048 + (fo + 1) * 128, ts_], r=[R["proj"]], w=[m_t])
                    fs = slice(fo * 128, (fo + 1) * 128)
                    pa, pb, pc = pa2[fo % 2], pb2[fo % 2], pc2[fo % 2]
                    t1, t2 = tA[fo % 2], tB[fo % 2]
                    P.op("pe", [lambda e, kc=kc, fs=fs: e.matmul(pa.h[:], lhsT=wa.h[:, kc, fs], rhs=ya_t.h[:, kc, :], start=(kc == 0), stop=(kc == 5)) for kc in range(6)],
                         r=[wa, ya_t], w=[pa])
                    P.op("pe", [lambda e, kc=kc, fs=fs: e.matmul(pb.h[:], lhsT=wbt.h[:, kc, fs], rhs=yb_t.h[:, kc, :], start=(kc == 0), stop=(kc == 3)) for kc in range(4)],
                         r=[wbt, yb_t], w=[pb])
                    P.op("pe", [lambda e, kc=kc, fs=fs: e.matmul(pc.h[:], lhsT=wc.h[:, kc, fs], rhs=yc_t.h[:, kc, :], start=(kc == 0), stop=(kc == 5)) for kc in range(6)],
                         r=[wc, yc_t], w=[pc])
                    P.op("dve", lambda e, m_t=m_t: e.tensor_tensor(out=t1.h[:], in0=pa.h[:], in1=m_t.h[:, 0, :], op=ALU.mult), r=[pa, m_t], w=[t1])
                    P.op("dve", lambda e, m_t=m_t: e.tensor_tensor(out=t2.h[:], in0=pb.h[:], in1=m_t.h[:, 1, :], op=ALU.mult), r=[pb, m_t], w=[t2])
                    P.op("dve", lambda e: e.tensor_tensor(out=t1.h[:], in0=t1.h[:], in1=t2.h[:], op=ALU.add), r=[t1, t2], w=[t1])
                    P.op("dve", lambda e, m_t=m_t: e.tensor_tensor(out=t2.h[:], in0=pc.h[:], in1=m_t.h[:, 2, :], op=ALU.mult), r=[pc, m_t], w=[t2])
                    P.op("dve", lambda e, fo=fo: e.tensor_tensor(out=mT.h[:, fo, :], in0=t1.h[:], in1=t2.h[:], op=ALU.add), r=[t1, t2], w=[mT])
                for s in range(4):
                    tok0 = tt * 512 + s * 128
                    P.dma("sp", xt[s].h[:], xsrc[tok0:tok0 + 128, :], r=[R["xres"]], w=[xt[s]])
                for cbk in range(8):
                    w_t = wo[nwo % 2]; nwo += 1
                    cs = slice(cbk * 256, (cbk + 1) * 256)
                    P.dma("sp", w_t.h[:], w_out_b[l, :, cs].rearrange("(kc p) n -> p kc n", p=128), r=[R["wb"]], w=[w_t])
                    for s in range(4):
                        p_o = po[s % 2]
                        P.op("pe", [lambda e, kc=kc, s=s, p_o=p_o, w_t=w_t: e.matmul(p_o.h[:, 0:256], lhsT=mT.h[:, kc, s * 128:(s + 1) * 128], rhs=w_t.h[:, kc, :],
                                                                                    start=(kc == 0), stop=(kc == 15)) for kc in range(16)],
                             r=[mT, w_t], w=[p_o])
                        tq = tq_[s % 2]
                        P.op("dve", lambda e, p_o=p_o, cs=cs, tq=tq: e.tensor_tensor(out=tq.h[:, 0:256], in0=p_o.h[:, 0:256], in1=gate.h[:, cs], op=ALU.mult),
                             r=[p_o, gate], w=[tq])
                        P.op("dve", lambda e, s=s, cs=cs, tq=tq: e.tensor_tensor(out=xt[s].h[:, cs], in0=xt[s].h[:, cs], in1=tq.h[:, 0:256], op=ALU.add),
                             r=[xt[s], tq], w=[xt[s]])
                for s in range(4):
                    tok0 = tt * 512 + s * 128
                    x_t = xt[s]
                    if not last:
                        P.dma("pool", xres[tok0:tok0 + 128, :], x_t.h[:], r=[x_t], w=[R["xres"]])
                    else:
                        P.op("act", lambda e, x_t=x_t: e.activation(out=junk.h[:], in_=x_t.h[:], func=AF.Square, accum_out=ss.h[:]), r=[x_t], w=[junk, ss])
                        P.op("act", lambda e: e.activation(out=ss.h[:], in_=ss.h[:], func=AF.Sqrt, scale=1.0 / D, bias=EPS), r=[ss], w=[ss])
                        P.op("dve", lambda e: e.reciprocal(out=ss.h[:], in_=ss.h[:]), r=[ss], w=[ss])
                        P.op("dve", lambda e, x_t=x_t: e.scalar_tensor_tensor(out=x_t.h[:], in0=x_t.h[:], scalar=ss.h[:, 0:1], in1=fnw.h[:], op0=ALU.mult, op1=ALU.mult),
                             r=[x_t, ss, fnw], w=[x_t])
                        P.dma("pool", y_out[tok0:tok0 + 128, :], x_t.h[:], r=[x_t])
        finish()

    if "prep" in phases:
        phase_cast()
        phase_mod()
    if "filt" in phases:
        phase_filt("s")
        phase_filt("p")
    for l in layers:
        if "proj" in phases:
            phase_proj(l)
        if "attn" in phases:
            lst = ExitStack()
            VZs = tile(lst, [128, 32, 512], BF16)
            VZp = tile(lst, [128, 2, 2048], BF16)
            for (wh, smp) in DEBUG.get("attn", (("A", True), ("A", False), ("C", True), ("C", False))):
                phase_attn(l, wh, smp, conv_args=((("s", VZs) if smp else ("p", VZp)) if wh == "A" else None))
            phase_hyena(l, "s", VZs)
            phase_hyena(l, "p", VZp)
            lst.close()
        if "merge" in phases:
            phase_merge(l)
    P.close()
    return nc, P


_CONSTS = None
_NC = None


def _get_nc():
    global _NC
    if _NC is None:
        _NC = build()[0]
    return _NC


def _make_in_maps(inp):
    global _CONSTS
    if _CONSTS is None:
        _CONSTS = _consts()
    f = lambda a: np.ascontiguousarray(np.asarray(a, dtype=np.float32))
    shared = {}
    for k in ("c_ctx", "norm_w", "w_ada", "b_ada", "w_in", "a_sink", "hy_conv_w", "hy_conv_b", "hy_w1", "hy_b1", "hy_w2", "hy_b2",
              "hy_freq", "hy_w3", "hy_skip", "w_up_a", "w_up_b", "w_up_c", "w_out", "final_norm_w"):
        shared[k] = f(inp[k])
    shared["hy_decay"] = f(inp["hy_decay"]).reshape(2, 2048)
    rpb = f(inp["c_rpb"])
    ni, ne = [], []
    for l in range(2):
        a, b = _na_tables(rpb[l])
        ni.append(a.reshape(128, -1)); ne.append(b.reshape(4, 128, -1))
    shared["nab_i"] = np.ascontiguousarray(np.stack(ni)); shared["nab_e"] = np.ascontiguousarray(np.stack(ne))
    shared.update(_CONSTS)
    xs = f(inp["x_sample"]); xp = f(inp["x_prompt"]); c = f(inp["c"])
    cak = f(inp["cache_a_k"]); cav = f(inp["cache_a_v"]); cck = f(inp["cache_c_k"]); ccv = f(inp["cache_c_v"])
    maps = []
    for i in range(8):
        m = dict(shared)
        m["xin"] = np.ascontiguousarray(np.concatenate([xs[i], xp[4 * i:4 * i + 4].reshape(NP, D)], axis=0))
        m["c_lat"] = np.ascontiguousarray(c[i])
        m["cak"] = np.ascontiguousarray(cak[i].reshape(2, 512, 256)); m["cav"] = np.ascontiguousarray(cav[i].reshape(2, 512, 256))
        m["cck"] = np.ascontiguousarray(cck[i].reshape(2, 512, 768)); m["ccv"] = np.ascontiguousarray(ccv[i].reshape(2, 512, 768))
        maps.append(m)
    return maps


def kernel(**inp):
    nc = _get_nc()
    maps = _make_in_maps(inp)
    res = run_bass_kernel_spmd(nc, maps, core_ids=list(range(8)))
    rs = res.results
    y_prompt = np.concatenate([r["y"][NS:].reshape(4, 256, D) for r in rs], axis=0)
    y_sample = np.stack([r["y"][:NS] for r in rs], axis=0)
    nak = np.concatenate([r["nak"].reshape(4, 2, 256, 4, 64) for r in rs], axis=0)
    nav = np.concatenate([r["nav"].reshape(4, 2, 256, 4, 64) for r in rs], axis=0)
    nck = np.concatenate([r["nck"].reshape(4, 2, 256, 12, 64) for r in rs], axis=0)
    ncv = np.concatenate([r["ncv"].reshape(4, 2, 256, 12, 64) for r in rs], axis=0)
    return (y_prompt.astype(np.float32), y_sample.astype(np.float32), nak.astype(np.float32), nav.astype(np.float32),
            nck.astype(np.float32), ncv.astype(np.float32))
```
